# Optimizing a Trainium2 kernel written in Bass

```python
import jax, jax.numpy as jnp
from jax import lax
import numpy as np

D_MODEL = 2048
BATCH = 8
SEQ = 4096
DEPTH = 1
DEC_BATCH = 1
DEC_SEQ = 8192
PAST_LEN = 128

N_META = 16
GRID_W = 64
NA_HEADS = 16
NA_HEAD_DIM = 64
NA_WIDTH = NA_HEADS * NA_HEAD_DIM
NA_WIN_H_MAX = 8
NA_WIN_W = 16
RET_HEADS = 8
RET_HEAD_DIM = 128
RET_WIDTH = RET_HEADS * RET_HEAD_DIM
RET_CHUNK = 128
ROPE_BASE = 10000.0
MIX_WIDTH = NA_WIDTH + RET_WIDTH
IN_WIDTH = 3 * NA_WIDTH + 4 * RET_WIDTH
D_FF = -(-(8 * D_MODEL) // (3 * 256)) * 256
LN_EPS = 1e-5
DEEPNORM_ALPHA = (2.0 * DEPTH) ** 0.25
DEEPNORM_BETA = (8.0 * DEPTH) ** -0.25

kernel_name = "hybrid_natten_retention_encoder"


def layer_norm(x, g, b):
    xf = x.astype(jnp.float32)
    mu = jnp.mean(xf, axis=-1, keepdims=True)
    var = jnp.mean(jnp.square(xf - mu), axis=-1, keepdims=True)
    y = (xf - mu) * lax.rsqrt(var + LN_EPS)
    return (y * g.astype(jnp.float32) + b.astype(jnp.float32)).astype(x.dtype)


def rope(x, pos):
    half = x.shape[-1] // 2
    inv = ROPE_BASE ** (-jnp.arange(half, dtype=jnp.float32) / half)
    ang = pos[:, None] * inv[None, :]
    c = jnp.cos(ang)[None, :, None, :]
    s = jnp.sin(ang)[None, :, None, :]
    xf = x.astype(jnp.float32)
    x1, x2 = xf[..., :half], xf[..., half:]
    return jnp.concatenate([x1 * c - x2 * s, x1 * s + x2 * c], axis=-1).astype(x.dtype)


def neighborhood_attention(q, k, v, rpb):
    B, H, L, d = q.shape
    n = L - N_META
    rows = n // GRID_W
    win_h = min(NA_WIN_H_MAX, rows)
    q = q * (d ** -0.5)
    qm, km, vm = q[:, :, :N_META], k[:, :, :N_META], v[:, :, :N_META]
    qg = q[:, :, N_META:].reshape(B, H, rows, GRID_W, d)
    kg = k[:, :, N_META:].reshape(B, H, rows, GRID_W, d)
    vg = v[:, :, N_META:].reshape(B, H, rows, GRID_W, d)

    s_mm = jnp.einsum('bhmd,bhnd->bhmn', qm, km).astype(jnp.float32)
    o_meta = jnp.einsum('bhmn,bhnd->bhmd', jax.nn.softmax(s_mm, axis=-1).astype(v.dtype), vm)

    cols = np.arange(GRID_W)
    cs = np.clip(cols - NA_WIN_W // 2, 0, GRID_W - NA_WIN_W)
    col_idx = cs[:, None] + np.arange(NA_WIN_W)[None, :]
    dc_idx = col_idx - cols[:, None] + (NA_WIN_W - 1)
    rpb_cols = rpb[:, :, dc_idx]

    def one_row(r):
        rs = jnp.clip(r - win_h // 2, 0, rows - win_h)
        q_row = lax.dynamic_index_in_dim(qg, r, axis=2, keepdims=False)
        k_band = lax.dynamic_slice_in_dim(kg, rs, win_h, axis=2)
        v_band = lax.dynamic_slice_in_dim(vg, rs, win_h, axis=2)
        k_win = k_band[:, :, :, col_idx]
        v_win = v_band[:, :, :, col_idx]
        dr_idx = rs + jnp.arange(win_h) - r + (NA_WIN_H_MAX - 1)
        bias = jnp.take(rpb_cols, dr_idx, axis=1)
        bias = jnp.transpose(bias, (0, 2, 1, 3)).astype(jnp.float32)
        s_win = jnp.einsum('bhcd,bhicjd->bhcij', q_row, k_win).astype(jnp.float32) + bias
        s_win = s_win.reshape(B, H, GRID_W, win_h * NA_WIN_W)
        s_meta = jnp.einsum('bhcd,bhmd->bhcm', q_row, km).astype(jnp.float32)
        p = jax.nn.softmax(jnp.concatenate([s_meta, s_win], axis=-1), axis=-1).astype(v.dtype)
        p_meta = p[..., :N_META]
        p_win = p[..., N_META:].reshape(B, H, GRID_W, win_h, NA_WIN_W)
        return (jnp.einsum('bhcm,bhmd->bhcd', p_meta, vm)
                + jnp.einsum('bhcij,bhicjd->bhcd', p_win, v_win))

    o_rows = lax.map(one_row, jnp.arange(rows))
    o_grid = jnp.transpose(o_rows, (1, 2, 0, 3, 4)).reshape(B, H, n, d)
    return jnp.concatenate([o_meta, o_grid], axis=2)


def retention_direction(q, k, v, log_gamma, include_diag):
    B, H, Lp, dk = q.shape
    dv = v.shape[-1]
    C = RET_CHUNK
    nC = Lp // C

    def chunks(t):
        return jnp.transpose(t.reshape(B, H, nC, C, t.shape[-1]), (2, 0, 1, 3, 4))

    posn = np.arange(C)
    diffn = posn[:, None] - posn[None, :]
    mask = (diffn >= 0) if include_diag else (diffn > 0)
    diff_pos = jnp.asarray(np.maximum(diffn, 0), dtype=jnp.float32)
    decay_intra = jnp.where(mask[None], jnp.exp(diff_pos[None] * log_gamma[:, None, None]), 0.0)
    pos = jnp.arange(C, dtype=jnp.float32)
    xi = jnp.exp((pos[None, :] + 1.0) * log_gamma[:, None])[..., None]
    zeta = jnp.exp((C - 1.0 - pos[None, :]) * log_gamma[:, None])[..., None]
    chunk_decay = jnp.exp(C * log_gamma)[:, None, None]

    def step(S, qkv):
        qc, kc, vc = qkv
        qf, kf, vf = qc.astype(jnp.float32), kc.astype(jnp.float32), vc.astype(jnp.float32)
        s = jnp.einsum('bhid,bhjd->bhij', qf, kf) * decay_intra
        o = jnp.einsum('bhij,bhje->bhie', s, vf) + jnp.einsum('bhid,bhde->bhie', qf * xi, S)
        S = S * chunk_decay + jnp.einsum('bhjd,bhje->bhde', kf * zeta, vf)
        return S, o

    S0 = jnp.zeros((B, H, dk, dv), jnp.float32)
    _, o = lax.scan(step, S0, (chunks(q), chunks(k), chunks(v)))
    return jnp.transpose(o, (1, 2, 0, 3, 4)).reshape(B, H, Lp, dv)


def bidirectional_retention(q, k, v, lg_f, lg_b):
    L = q.shape[2]
    pad = (-L) % RET_CHUNK
    cfg = ((0, 0), (0, 0), (pad, 0), (0, 0))
    qp, kp, vp = jnp.pad(q, cfg), jnp.pad(k, cfg), jnp.pad(v, cfg)
    o_f = retention_direction(qp, kp, vp, lg_f, True)
    o_b = retention_direction(qp[:, :, ::-1], kp[:, :, ::-1], vp[:, :, ::-1], lg_b, False)[:, :, ::-1]
    return (o_f + o_b)[:, :, pad:]


def encoder_layer(h, w_in, na_rpb, ret_decay_f, ret_decay_b, ret_gn_g, w_out,
                  ln1_g, ln1_b, w_ffn_gate, w_ffn_up, w_ffn_down, ln2_g, ln2_b):
    B, L, _ = h.shape
    proj = h @ w_in
    sizes = [NA_WIDTH] * 3 + [RET_WIDTH] * 4
    splits = [int(s) for s in np.cumsum(sizes)[:-1]]
    qa, ka, va, qr, kr, vr, gr = jnp.split(proj, splits, axis=-1)

    def na_heads(t):
        return jnp.transpose(t.reshape(B, L, NA_HEADS, NA_HEAD_DIM), (0, 2, 1, 3))
    o_na = neighborhood_attention(na_heads(qa), na_heads(ka), na_heads(va), na_rpb)
    o_na = jnp.transpose(o_na, (0, 2, 1, 3)).reshape(B, L, NA_WIDTH)

    pos = jnp.arange(L, dtype=jnp.float32)
    qh = jnp.transpose(rope(qr.reshape(B, L, RET_HEADS, RET_HEAD_DIM), pos), (0, 2, 1, 3))
    kh = jnp.transpose(rope(kr.reshape(B, L, RET_HEADS, RET_HEAD_DIM), pos), (0, 2, 1, 3)) * (RET_HEAD_DIM ** -0.5)
    vh = jnp.transpose(vr.reshape(B, L, RET_HEADS, RET_HEAD_DIM), (0, 2, 1, 3))
    lg_f = jax.nn.log_sigmoid(ret_decay_f.astype(jnp.float32))
    lg_b = jax.nn.log_sigmoid(ret_decay_b.astype(jnp.float32))
    o_ret = jnp.transpose(bidirectional_retention(qh, kh, vh, lg_f, lg_b), (0, 2, 1, 3))
    mu = jnp.mean(o_ret, axis=-1, keepdims=True)
    var = jnp.mean(jnp.square(o_ret - mu), axis=-1, keepdims=True)
    o_ret = ((o_ret - mu) * lax.rsqrt(var + LN_EPS)).reshape(B, L, RET_WIDTH) * ret_gn_g.astype(jnp.float32)
    o_ret = (jax.nn.silu(gr.astype(jnp.float32)) * o_ret).astype(h.dtype)

    mix = jnp.concatenate([o_na, o_ret], axis=-1) @ w_out
    h = layer_norm(DEEPNORM_ALPHA * h + mix, ln1_g, ln1_b)
    ffn = (jax.nn.silu(h @ w_ffn_gate) * (h @ w_ffn_up)) @ w_ffn_down
    return layer_norm(DEEPNORM_ALPHA * h + ffn, ln2_g, ln2_b)


def trunk(x, meta_tokens, ln_in_g, ln_in_b, w_in, na_rpb, ret_decay_f, ret_decay_b, ret_gn_g,
          w_out, ln1_g, ln1_b, w_ffn_gate, w_ffn_up, w_ffn_down, ln2_g, ln2_b):
    B = x.shape[0]
    meta = jnp.broadcast_to(meta_tokens[None].astype(x.dtype), (B, N_META, x.shape[-1]))
    h = layer_norm(jnp.concatenate([meta, x], axis=1), ln_in_g, ln_in_b)
    for l in range(DEPTH):
        h = encoder_layer(h, w_in[l], na_rpb[l], ret_decay_f[l], ret_decay_b[l], ret_gn_g[l], w_out[l],
                          ln1_g[l], ln1_b[l], w_ffn_gate[l], w_ffn_up[l], w_ffn_down[l], ln2_g[l], ln2_b[l])
    return h[:, N_META:]


def setup_inputs(seed: int = 0) -> dict:
    key = jax.random.key(seed)
    ks = jax.random.split(key, 20)
    f32 = jnp.float32
    nrm = lambda k, shape: jax.random.normal(k, shape, f32)

    col_scale = np.ones((IN_WIDTH,), np.float32)
    col_scale[2 * NA_WIDTH:3 * NA_WIDTH] = DEEPNORM_BETA
    col_scale[3 * NA_WIDTH + 2 * RET_WIDTH:3 * NA_WIDTH + 3 * RET_WIDTH] = DEEPNORM_BETA
    w_in = nrm(ks[3], (DEPTH, D_MODEL, IN_WIDTH)) * (D_MODEL ** -0.5) * jnp.asarray(col_scale)

    decay_base = jnp.asarray(np.log(2.0 ** (5.0 + np.arange(RET_HEADS)) - 1.0), dtype=f32)
    return {
        "x_prompt": nrm(ks[0], (BATCH, SEQ, D_MODEL)),
        "x_sample": nrm(ks[1], (DEC_BATCH, DEC_SEQ, D_MODEL)),
        "meta_tokens": nrm(ks[2], (N_META, D_MODEL)),
        "ln_in_g": 1.0 + 0.02 * nrm(ks[4], (D_MODEL,)),
        "ln_in_b": 0.02 * nrm(ks[5], (D_MODEL,)),
        "w_in": w_in,
        "na_rpb": 0.1 * nrm(ks[6], (DEPTH, NA_HEADS, 2 * NA_WIN_H_MAX - 1, 2 * NA_WIN_W - 1)),
        "ret_decay_f": decay_base[None] + 0.01 * nrm(ks[7], (DEPTH, RET_HEADS)),
        "ret_decay_b": decay_base[None] + 0.01 * nrm(ks[8], (DEPTH, RET_HEADS)),
        "ret_gn_g": 1.0 + 0.02 * nrm(ks[9], (DEPTH, RET_WIDTH)),
        "w_out": nrm(ks[10], (DEPTH, MIX_WIDTH, D_MODEL)) * (MIX_WIDTH ** -0.5) * DEEPNORM_BETA,
        "ln1_g": 1.0 + 0.02 * nrm(ks[11], (DEPTH, D_MODEL)),
        "ln1_b": 0.02 * nrm(ks[12], (DEPTH, D_MODEL)),
        "w_ffn_gate": nrm(ks[13], (DEPTH, D_MODEL, D_FF)) * (D_MODEL ** -0.5) * DEEPNORM_BETA,
        "w_ffn_up": nrm(ks[14], (DEPTH, D_MODEL, D_FF)) * (D_MODEL ** -0.5) * DEEPNORM_BETA,
        "w_ffn_down": nrm(ks[15], (DEPTH, D_FF, D_MODEL)) * (D_FF ** -0.5) * DEEPNORM_BETA,
        "ln2_g": 1.0 + 0.02 * nrm(ks[16], (DEPTH, D_MODEL)),
        "ln2_b": 0.02 * nrm(ks[17], (DEPTH, D_MODEL)),
    }


def reference(x_prompt, x_sample, meta_tokens, ln_in_g, ln_in_b, w_in, na_rpb, ret_decay_f, ret_decay_b,
              ret_gn_g, w_out, ln1_g, ln1_b, w_ffn_gate, w_ffn_up, w_ffn_down, ln2_g, ln2_b):
    y_prompt = trunk(x_prompt, meta_tokens, ln_in_g, ln_in_b, w_in, na_rpb, ret_decay_f, ret_decay_b,
                     ret_gn_g, w_out, ln1_g, ln1_b, w_ffn_gate, w_ffn_up, w_ffn_down, ln2_g, ln2_b)
    y_sample = trunk(x_sample, meta_tokens, ln_in_g, ln_in_b, w_in, na_rpb, ret_decay_f, ret_decay_b,
                     ret_gn_g, w_out, ln1_g, ln1_b, w_ffn_gate, w_ffn_up, w_ffn_down, ln2_g, ln2_b)
    return (y_prompt, y_sample)
```

```python
import contextlib
import os
import numpy as np
import concourse.bass as bass
import concourse.mybir as mybir
from concourse.bass_utils import run_bass_kernel_spmd

F32 = mybir.dt.float32
BF16 = mybir.dt.bfloat16
ALU = mybir.AluOpType
AF = mybir.ActivationFunctionType

D = 2048
KC = 16
C = 128
N_META = 16
ALPHA = float(2.0 ** 0.25)
EPS = 1e-5
NEG = -30000.0
NRING = 24


class Cfg:
    def __init__(self, NT_P=32, NT_S=64, DFF=5632, debug=False):
        self.NT_P, self.NT_S, self.DFF, self.debug = NT_P, NT_S, DFF, debug
        self.NS_OWN = NT_S // 8
        self.NWIN = self.NS_OWN + 6
        self.NREST = NT_S - (self.NS_OWN + 3)
        self.NSL = self.NWIN + self.NREST
        self.NOWN = NT_P + self.NS_OWN
        self.NNA = NT_P + self.NWIN
        self.NT1 = 1 + NT_P + self.NSL
        self.FC = DFF // 128


class Buf:
    __slots__ = ("name", "w", "r")

    def __init__(self, name):
        self.name = name
        self.w = None
        self.r = {}


class Tile:
    def __init__(self, t, name):
        self.t = t
        self.b = Buf(name)


class Eng:
    def __init__(self, name, h):
        self.name, self.h = name, h
        self.sem = None
        self.cnt = 0
        self.pending = False
        self.waited = {}


class KB:
    def __init__(self):
        self.nc = bass.Bass("TRN2", target_bir_lowering=False)
        self.es = contextlib.ExitStack()
        nc = self.nc
        self.eng = {n: Eng(n, h) for n, h in [("pe", nc.tensor), ("act", nc.scalar), ("dve", nc.vector),
                                              ("pool", nc.gpsimd), ("sp", nc.sync)]}
        self.nsem = 0
        self.dq = {}
        for q in ("sp", "pool"):
            self.dq[q] = [[self.newsem() for _ in range(NRING)], 0]
        self.ndma = 0
        self.phase = None
        self.pstack = []
        self.recording = None

    def newsem(self):
        self.nsem += 1
        return (self.nsem, self.es.enter_context(self.nc.semaphore("s%d" % self.nsem)))

    def sb(self, name, shape, dt):
        st = self.phase if self.phase is not None else self.es
        self.nalloc = getattr(self, "nalloc", 0) + 1
        return Tile(st.enter_context(self.nc.sbuf_tensor("sb%d_%s" % (self.nalloc, name), list(shape), dt)), name)

    def push(self):
        self.pstack.append(contextlib.ExitStack())
        self.phase = self.pstack[-1]

    def pop(self):
        if os.environ.get("KB_VERBOSE"):
            print("phase end: sbuf bytes remaining", self.nc.sbuf_bytes_remaining)
        self.barrier()
        self.pstack.pop().close()
        self.phase = self.pstack[-1] if self.pstack else None

    def close_all(self):
        while self.pstack:
            self.pstack.pop().close()
        self.phase = None

    def barrier(self):
        evs = []
        for e in self.eng.values():
            assert not e.pending
            if e.sem is not None and e.cnt > 0:
                evs.append((e.sem[0], e.sem[1], e.cnt))
        for q in ("sp", "pool"):
            ring, i = self.dq[q]
            n = len(ring)
            for s_ in range(min(i, n)):
                evs.append((ring[s_][0], ring[s_][1], 16 * ((i - 1 - s_) // n + 1)))
        for e in self.eng.values():
            for ev in evs:
                self._need(e, ev)

    def ps(self, name, shape, dt):
        return Tile(self.es.enter_context(self.nc.psum_tensor("ps_" + name, list(shape), dt)), name)

    def _need(self, e, ev):
        k, h, v = ev
        if e.waited.get(k, 0) >= v:
            return
        e.h.wait_ge(h, v)
        e.waited[k] = v

    def _deps(self, e, r, w):
        for b in r:
            if b.w is not None:
                self._need(e, b.w[0])
        for b in w:
            if b.w is not None and not (e.name == "pe" and b.w[1] == "pe"):
                self._need(e, b.w[0])
            for en, ev in b.r.items():
                self._need(e, ev)

    def op(self, en, fn, r=(), w=(), inc=True):
        if self.recording is not None:
            self.recording.append(("op", (en, fn, r, w, inc)))
            return None
        e = self.eng[en]
        r = [x.b if isinstance(x, Tile) else x for x in r]
        w = [x.b if isinstance(x, Tile) else x for x in w]
        self._deps(e, r, w)
        if e.sem is None or (e.cnt >= 8000 and not e.pending):
            e.sem = self.newsem()
            e.cnt = 0
        ins = fn(e.h)
        if inc:
            ins.then_inc(e.sem[1], 1)
            e.cnt += 1
            e.pending = False
            ev = (e.sem[0], e.sem[1], e.cnt)
        else:
            e.pending = True
            ev = (e.sem[0], e.sem[1], e.cnt + 1)
        for b in r:
            b.r[en] = ev
        for b in w:
            b.w = (ev, en)
            b.r = {}
        return ins

    def dma(self, q, out, in_, r=(), w=()):
        if self.recording is not None:
            self.recording.append(("dma", (q, out, in_, r, w)))
            return
        e = self.eng[q]
        r = [x.b if isinstance(x, Tile) else x for x in r]
        w = [x.b if isinstance(x, Tile) else x for x in w]
        self._deps(e, r, w)
        ring, i = self.dq[q]
        n = len(ring)
        key, h = ring[i % n]
        val = 16 * (i // n + 1)
        if i >= n:
            self._need(e, (key, h, val - 16))
        e.h.dma_start(out=out, in_=in_).then_inc(h, 16)
        self.dq[q][1] = i + 1
        ev = (key, h, val)
        self.ndma += 1
        for b in r:
            b.r["dma%d" % self.ndma] = ev
        for b in w:
            b.w = (ev, "dma")
            b.r = {}

    def replay(self, rec):
        for kind, a in rec:
            (self.op if kind == "op" else self.dma)(*a)

    def finish(self):
        for q in ("sp", "pool"):
            ring, i = self.dq[q]
            n = len(ring)
            for s in range(min(i, n)):
                cnt = (i - 1 - s) // n + 1
                self._need(self.eng[q], (ring[s][0], ring[s][1], 16 * cnt))
        for e in self.eng.values():
            assert not e.pending, e.name


def build(cfg):
    kb = KB()
    nc = kb.nc
    NT_P, NT_S, DFF, FC = cfg.NT_P, cfg.NT_S, cfg.DFF, cfg.FC
    NS_OWN, NWIN, NREST, NSL, NOWN, NNA, NT1 = cfg.NS_OWN, cfg.NWIN, cfg.NREST, cfg.NSL, cfg.NOWN, cfg.NNA, cfg.NT1
    NVP = cfg.NVP

    def din(name, shape, dt=F32):
        return nc.dram_tensor(name, list(shape), dt, kind="ExternalInput").ap()

    def dscr(name, shape, dt, dbg=False):
        kind = "ExternalOutput" if (dbg and cfg.debug) else "Internal"
        return nc.dram_tensor(name, list(shape), dt, kind=kind).ap()

    xm = din("xm", [128, D])
    xp = din("xp", [NT_P * 128, D])
    xs = din("xs", [NSL * 128, D])
    w_in = din("w_in", [D, 7168])
    w_out = din("w_out", [D, D])
    w_g = din("w_g", [D, DFF])
    w_u = din("w_u", [D, DFF])
    w_d = din("w_d", [DFF, D])
    lnt = din("lnt", [6, 128, D])
    gng_d = din("gng", [128, 1024])
    dec_d = din("dec", [128, 16])
    ident_d = din("ident", [128, 128])
    rope_d = din("rope", [NT1, 128, 4, 64])
    atc_d = din("atc", [128, 4, 128])
    xz_d = din("xz", [128, 4])
    sic_d = din("sic", [128, 4, 1 + NSL])
    sip_d = din("sip", [128, 2])
    biasu_d = din("biasu", [128, 7, 16, 128])
    maskp_d = din("maskp", [128, NVP, 128])
    masks_d = din("masks", [128, NS_OWN * 7, 128])
    y_out = nc.dram_tensor("y", [NOWN * 128, D], F32, kind="ExternalOutput").ap()

    wb_in = dscr("wb_in", [D, 7168], BF16)
    wb_out = dscr("wb_out", [D, D], BF16)
    wb_g = dscr("wb_g", [D, DFF], BF16)
    wb_u = dscr("wb_u", [D, DFF], BF16)
    wb_d = dscr("wb_d", [DFF, D], BF16)
    qaT_s = dscr("qaT_s", [NOWN, 128, 8, 128], BF16, True)
    qrT_s = dscr("qrT_s", [NOWN, 3, 128, 8, 128], BF16, True)
    g_s = dscr("g_s", [NOWN, 128, 1024], F32, True)
    krT_s = dscr("krT_s", [NOWN, 128, 8, 128], BF16, True)
    kzf_s = dscr("kzf_s", [NOWN, 128, 1024], BF16, True)
    kzb_s = dscr("kzb_s", [NOWN, 128, 1024], BF16, True)
    vr_s = dscr("vr_s", [NOWN, 128, 1024], BF16, True)
    kaT_s = dscr("kaT_s", [NNA, 128, 8, 128], BF16, True)
    va_s = dscr("va_s", [NNA, 128, 1024], BF16, True)
    sb_s = dscr("sb_s", [NOWN, 128, 8, 128], BF16, True)
    mixT_s = dscr("mixT_s", [NOWN, 128, 16, 128], BF16, True)
    h_s = dscr("h_s", [NOWN, 128, D], F32, True)
    h1_s = dscr("h1_s", [NOWN, 128, D], F32, True)
    h1T_s = dscr("h1T_s", [NOWN, 128, 16, 128], BF16, True)
    sinit_s = dscr("sinit_s", [3, 128, 1024], F32, True)

    def bufs(prefix, n):
        return [Buf("%s%d" % (prefix, i)) for i in range(n)]

    B_qaT, B_g, B_krT, B_kzf, B_kzb, B_vr = (bufs(p, NOWN) for p in ("qaT", "g", "krT", "kzf", "kzb", "vr"))
    B_qrT = [bufs("qrT%d_" % i, 3) for i in range(NOWN)]
    B_kaT, B_va = bufs("kaT", NNA), bufs("va", NNA)
    B_sb, B_mixT, B_h, B_h1, B_h1T = (bufs(p, NOWN) for p in ("sb", "mixT", "h", "h1", "h1T"))

    cast_jobs = []

    def cast_weight(src, dst, rows, cols, name):
        chunks = []
        per = max(1, 8192 // (cols // 512))
        r0 = 0
        while r0 < rows:
            r1 = min(rows, r0 + per)
            b = Buf("%s_%d" % (name, r0))
            cast_jobs.append(lambda r0=r0, r1=r1, b=b: kb.dma("pool", dst[r0:r1, :].rearrange("r (a b) -> (r a) b", b=512),
                                                            src[r0:r1, :].rearrange("r (a b) -> (r a) b", b=512), w=[b]))
            chunks.append(b)
            r0 = r1
        return chunks

    WB_out = cast_weight(w_out, wb_out, D, D, "wbout")
    WB_g = cast_weight(w_g, wb_g, D, DFF, "wbg")
    WB_u = cast_weight(w_u, wb_u, D, DFF, "wbu")
    WB_d = cast_weight(w_d, wb_d, DFF, D, "wbd")

    WB_in = {}

    def cast_in(lst):
        for u_, c0_ in lst:
            WB_in[u_] = Buf("wbin_" + u_)
            kb.dma("pool", wb_in[:, c0_:c0_ + 1024], w_in[:, c0_:c0_ + 1024], w=[WB_in[u_]])

    cast_in((("va", 2048), ("ka", 1024), ("vr", 5120), ("kr", 4096)))

    ident_f = kb.sb("ident_f", [128, 128], F32)
    ident = kb.sb("ident", [128, 128], BF16)
    kb.dma("sp", ident_f.t[:], ident_d[:, :], w=[ident_f])
    kb.op("dve", lambda e: e.tensor_copy(out=ident.t[:], in_=ident_f.t[:]), r=[ident_f], w=[ident])

    lng = kb.sb("lng", [128, D], F32)
    lnb = kb.sb("lnb", [128, D], F32)
    stat = kb.sb("stat", [128, 4, 6], F32)
    mv = kb.sb("mv", [128, 2], F32)
    rstd = kb.sb("rstd", [128, 1], F32)
    nmr = kb.sb("nmr", [128, 1], F32)
    lnx = kb.sb("lnx", [128, D], F32)
    kb.push()
    dec = kb.sb("dec", [128, 16], F32)
    kb.dma("sp", dec.t[:], dec_d[:, :], w=[dec])
    lg = kb.sb("lg", [128, 16], F32)
    tu = kb.sb("tu", [128, 16], F32)
    tw = kb.sb("tw", [128, 16], F32)
    td = kb.sb("td", [128, 16], F32)
    tl = kb.sb("tl", [128, 16], F32)
    kb.op("act", lambda e: e.activation(out=tu.t[:], in_=dec.t[:], func=AF.Exp, scale=-1.0), r=[dec], w=[tu])
    kb.op("dve", lambda e: e.tensor_scalar(out=tw.t[:], in0=tu.t[:], scalar1=1.0, scalar2=None, op0=ALU.add), r=[tu], w=[tw])
    kb.op("dve", lambda e: e.tensor_scalar(out=td.t[:], in0=tw.t[:], scalar1=-1.0, scalar2=1e-30, op0=ALU.add, op1=ALU.max), r=[tw], w=[td])
    kb.op("act", lambda e: e.activation(out=tl.t[:], in_=tw.t[:], func=AF.Ln), r=[tw], w=[tl])
    kb.op("dve", lambda e: e.reciprocal(out=td.t[:], in_=td.t[:]), r=[td], w=[td])
    kb.op("dve", lambda e: e.tensor_tensor(out=td.t[:], in0=tu.t[:], in1=td.t[:], op=ALU.mult), r=[tu, td], w=[td])
    kb.op("dve", lambda e: e.scalar_tensor_tensor(out=lg.t[:], in0=tl.t[:], scalar=-1.0, in1=td.t[:], op0=ALU.mult, op1=ALU.mult),
          r=[tl, td], w=[lg])

    kb.recording = []
    xz = kb.sb("xz", [128, 4], F32)
    kb.dma("sp", xz.t[:], xz_d[:, :], w=[xz])
    xif, xib, zef, zeb, gcf, gcb = (kb.sb(n, [128, 8], F32) for n in ("xif", "xib", "zef", "zeb", "gcf", "gcb"))
    for (dst, col, lo) in ((xif, 0, 0), (xib, 1, 8), (zef, 2, 0), (zeb, 3, 8)):
        kb.op("dve", lambda e, dst=dst, col=col, lo=lo: e.tensor_scalar(out=dst.t[:], in0=lg.t[:, lo:lo + 8], scalar1=xz.t[:, col:col + 1],
                                                                      scalar2=None, op0=ALU.mult), r=[lg, xz], w=[dst])
        kb.op("act", lambda e, dst=dst: e.activation(out=dst.t[:], in_=dst.t[:], func=AF.Exp), r=[dst], w=[dst])
    for (dst, lo) in ((gcf, 0), (gcb, 8)):
        kb.op("act", lambda e, dst=dst, lo=lo: e.activation(out=dst.t[:], in_=lg.t[:, lo:lo + 8], func=AF.Exp, scale=float(C)), r=[lg], w=[dst])

    atc = kb.sb("atc", [128, 4, 128], F32)
    kb.dma("sp", atc.t[:], atc_d[:, :, :], w=[atc])
    AT = kb.sb("AT", [128, 8, 128], F32)
    att = kb.sb("att", [128, 2, 128], F32)
    for h in range(8):
        kb.op("act", lambda e, h=h: e.activation(out=att.t[:, 0, :], in_=atc.t[:, 0, :], func=AF.Exp, scale=lg.t[:, h:h + 1]), r=[atc, lg], w=[att])
        kb.op("act", lambda e, h=h: e.activation(out=att.t[:, 1, :], in_=atc.t[:, 2, :], func=AF.Exp, scale=lg.t[:, 8 + h:9 + h]), r=[atc, lg], w=[att])
        kb.op("dve", lambda e: e.tensor_tensor(out=att.t[:, 0, :], in0=att.t[:, 0, :], in1=atc.t[:, 1, :], op=ALU.mult), r=[att, atc], w=[att])
        kb.op("dve", lambda e: e.tensor_tensor(out=att.t[:, 1, :], in0=att.t[:, 1, :], in1=atc.t[:, 3, :], op=ALU.mult), r=[att, atc], w=[att])
        kb.op("dve", lambda e, h=h: e.tensor_tensor(out=AT.t[:, h, :], in0=att.t[:, 0, :], in1=att.t[:, 1, :], op=ALU.add), r=[att], w=[AT])

    sic = kb.sb("sic", [128, 4, 1 + NSL], F32)
    kb.dma("sp", sic.t[:], sic_d[:, :, :], w=[sic])
    sip = kb.sb("sip", [128, 2], F32)
    kb.dma("sp", sip.t[:], sip_d[:, :], w=[sip])
    sWf = kb.sb("sWf", [128, 8, 1 + NSL], F32)
    sWb = kb.sb("sWb", [128, 8, 1 + NSL], F32)
    sPf = kb.sb("sPf", [128, 8], F32)
    for h in range(8):
        kb.op("act", lambda e, h=h: e.activation(out=sWf.t[:, h, :], in_=sic.t[:, 0, :], func=AF.Exp, scale=lg.t[:, h:h + 1]), r=[sic, lg], w=[sWf])
        kb.op("dve", lambda e, h=h: e.tensor_tensor(out=sWf.t[:, h, :], in0=sWf.t[:, h, :], in1=sic.t[:, 1, :], op=ALU.mult), r=[sWf, sic], w=[sWf])
        kb.op("act", lambda e, h=h: e.activation(out=sWb.t[:, h, :], in_=sic.t[:, 2, :], func=AF.Exp, scale=lg.t[:, 8 + h:9 + h]), r=[sic, lg], w=[sWb])
        kb.op("dve", lambda e, h=h: e.tensor_tensor(out=sWb.t[:, h, :], in0=sWb.t[:, h, :], in1=sic.t[:, 3, :], op=ALU.mult), r=[sWb, sic], w=[sWb])
    kb.op("dve", lambda e: e.tensor_scalar(out=sPf.t[:], in0=lg.t[:, 0:8], scalar1=sip.t[:, 0:1], scalar2=None, op0=ALU.mult), r=[lg, sip], w=[sPf])
    kb.op("act", lambda e: e.activation(out=sPf.t[:], in_=sPf.t[:], func=AF.Exp), r=[sPf], w=[sPf])
    kb.op("dve", lambda e: e.tensor_scalar(out=sPf.t[:], in0=sPf.t[:], scalar1=sip.t[:, 1:2], scalar2=None, op0=ALU.mult), r=[sPf, sip], w=[sPf])

    late_consts = kb.recording
    kb.recording = None

    def load_ln(i):
        kb.dma("sp", lng.t[:], lnt[2 * i, :, :], w=[lng])
        kb.dma("sp", lnb.t[:], lnt[2 * i + 1, :, :], w=[lnb])

    PB = [kb.ps("pb%d" % i, [128, 512], F32) for i in range(6)]
    PT = [kb.ps("pt%d" % i, [128, 1024], BF16) for i in range(2)]
    ptc = [0]

    def layer_norm(src, dst_f32, dst_bf=None):
        for c4 in range(4):
            kb.op("dve", lambda e, c4=c4: e.bn_stats(out=stat.t[:, c4, :], in_=src.t[:, c4 * 512:(c4 + 1) * 512]), r=[src], w=[stat])
        kb.op("dve", lambda e: e.bn_aggr(out=mv.t[:], in_=stat.t[:]), r=[stat], w=[mv])
        kb.op("act", lambda e: e.activation(out=rstd.t[:], in_=mv.t[:, 1:2], func=AF.Sqrt, bias=EPS, scale=1.0), r=[mv], w=[rstd])
        kb.op("dve", lambda e: e.reciprocal(out=rstd.t[:], in_=rstd.t[:]), r=[rstd], w=[rstd])
        kb.op("dve", lambda e: e.scalar_tensor_tensor(out=nmr.t[:], in0=mv.t[:, 0:1], scalar=-1.0, in1=rstd.t[:], op0=ALU.mult, op1=ALU.mult),
              r=[mv, rstd], w=[nmr])
        kb.op("act", lambda e: e.activation(out=lnx.t[:], in_=src.t[:], func=AF.Identity, bias=nmr.t[:], scale=rstd.t[:]), r=[src, nmr, rstd], w=[lnx])
        kb.op("pool", lambda e: e.tensor_tensor(out=lnx.t[:], in0=lnx.t[:], in1=lng.t[:], op=ALU.mult), r=[lnx, lng], w=[lnx])
        if dst_f32 is None:
            kb.op("pool", lambda e: e.tensor_tensor(out=dst_bf.t[:], in0=lnx.t[:], in1=lnb.t[:], op=ALU.add), r=[lnx, lnb], w=[dst_bf])
            return
        kb.op("dve", lambda e: e.tensor_tensor(out=dst_f32.t[:], in0=lnx.t[:], in1=lnb.t[:], op=ALU.add), r=[lnx, lnb], w=[dst_f32])
        if dst_bf is not None:
            kb.op("act", lambda e: e.activation(out=dst_bf.t[:], in_=dst_f32.t[:], func=AF.Copy), r=[dst_f32], w=[dst_bf])

    def transpose_cols(src, dstT_ap_fn, nblk, wbuf, evac="dve", src_off=0):
        for g0 in range(0, nblk, 8):
            n = min(8, nblk - g0)
            pt = PT[ptc[0] % 2]
            ptc[0] += 1
            for c in range(n):
                kb.op("pe", lambda e, c=c, g0=g0: e.transpose(out=pt.t[:, c * 128:(c + 1) * 128],
                                                                in_=src.t[:, src_off + (g0 + c) * 128: src_off + (g0 + c + 1) * 128],
                                                                identity=ident.t[:]),
                      r=[src, ident], w=[pt], inc=(c == n - 1))
            kb.op(evac, lambda e, g0=g0, n=n: e.tensor_copy(out=dstT_ap_fn(g0, n), in_=pt.t[:, 0:n * 128].rearrange("p (c t) -> p c t", t=128)),
                  r=[pt], w=[wbuf])


    if getattr(cfg, "stop", 99) == 0:
        kb.finish()
        kb.close_all()
        return kb
    load_ln(0)
    va_meta = kb.sb("va_meta", [16, 16, 66], BF16)
    kaT_meta = kb.sb("kaT_meta", [128, 8, 16], BF16)
    kb.op("dve", lambda e: e.memset(va_meta.t[:], 1.0), w=[va_meta])
    SA = [kb.sb("SA%d" % i, [128, 8, 128], F32) for i in range(3)]
    for t_ in SA:
        kb.op("pool", lambda e, t_=t_: e.memset(t_.t[:], 0.0), w=[t_])

    tiles = [dict(kind="meta", x=xm[:, :], rope=0)]
    for j in range(NT_P):
        tiles.append(dict(kind="own", x=xp[j * 128:(j + 1) * 128, :], rope=1 + j, own=j, na=j))
    for s_ in range(NSL):
        d = dict(x=xs[s_ * 128:(s_ + 1) * 128, :], rope=1 + NT_P + s_, slot=1 + s_)
        if s_ < NWIN:
            d["na"] = NT_P + s_
            if 3 <= s_ < 3 + NS_OWN:
                d["kind"] = "own"
                d["own"] = NT_P + (s_ - 3)
            else:
                d["kind"] = "halo"
        else:
            d["kind"] = "rest"
        tiles.append(d)
    UNITS = {"meta": ("va", "ka", "vr", "kr"), "own": ("va", "ka", "vr", "kr", "qa", "qr", "gr"),
             "halo": ("va", "ka", "vr", "kr"), "rest": ("vr", "kr")}
    UCOL = {"qa": 0, "ka": 1024, "va": 2048, "qr": 3072, "kr": 4096, "vr": 5120, "gr": 6144}
    NB1 = 4
    kb.push()
    xin = [kb.sb("xin%d" % i, [128, D], F32) for i in range(2)]
    NHB = 3
    hb = [kb.sb("hb%d" % i, [128, D], BF16) for i in range(NHB)]
    hTs = [kb.sb("hT%d" % i, [128, KC, 128], BF16) for i in range(2 * NB1)]
    wsl = [kb.sb("wsl%d" % i, [128, KC, 512], BF16) for i in range(3)]
    vtok = [kb.sb("vtok%d" % i, [128, 1024], BF16) for i in range(NB1)]
    stg = [kb.sb("stg%d" % i, [128, 1024], BF16) for i in range(3)]
    stgT = [kb.sb("stgT%d" % i, [128, 8, 128], BF16) for i in range(3)]
    ropet = [kb.sb("ropet%d" % i, [128, 4, 64], F32) for i in range(2 * NB1)]
    rof = [kb.sb("rof%d" % i, [128, 8, 128], F32) for i in range(2)]
    rt = [kb.sb("rt%d" % i, [128, 4, 64], F32) for i in range(4)]
    gst = [kb.sb("gst%d" % i, [128, 1024], F32) for i in range(1)]
    cnt = dict(stg=0, stgT=0, rof=0, gst=0, w=0, pb=0, x=0)

    def nxt(lst, key):
        v = lst[cnt[key] % len(lst)]
        cnt[key] += 1
        return v

    def store_T(src_bf, dst_ap, dbuf, src_off=0):
        tT = nxt(stgT, "stgT")
        transpose_cols(src_bf, lambda g0, n: tT.t[:, g0:g0 + n, :], 8, tT, src_off=src_off)
        kb.dma("pool", dst_ap, tT.t[:], r=[tT], w=[dbuf])
        return tT

    def rope_unit(pa, pb, tb, tblo, dst):
        for half, p in enumerate((pa, pb)):
            pv = p.t[:].rearrange("p (h d) -> p h d", d=128)
            x1, x2 = pv[:, :, 0:64], pv[:, :, 64:128]
            cc = tb.t[:, tblo:tblo + 1, :].broadcast_to([128, 4, 64])
            ss = tb.t[:, tblo + 1:tblo + 2, :].broadcast_to([128, 4, 64])
            o = dst.t[:, half * 4:(half + 1) * 4, :]
            kb.op("dve", lambda e: e.tensor_tensor(out=rt[0].t[:], in0=x1, in1=cc, op=ALU.mult), r=[p, tb], w=[rt[0]])
            kb.op("dve", lambda e: e.tensor_tensor(out=rt[1].t[:], in0=x2, in1=ss, op=ALU.mult), r=[p, tb], w=[rt[1]])
            kb.op("dve", lambda e: e.tensor_tensor(out=rt[2].t[:], in0=x1, in1=ss, op=ALU.mult), r=[p, tb], w=[rt[2]])
            kb.op("dve", lambda e: e.tensor_tensor(out=rt[3].t[:], in0=x2, in1=cc, op=ALU.mult), r=[p, tb], w=[rt[3]])
            kb.op("pool", lambda e: e.tensor_tensor(out=o[:, :, 0:64], in0=rt[0].t[:], in1=rt[1].t[:], op=ALU.subtract), r=[rt[0], rt[1]], w=[dst])
            kb.op("pool", lambda e: e.tensor_tensor(out=o[:, :, 64:128], in0=rt[2].t[:], in1=rt[3].t[:], op=ALU.add), r=[rt[2], rt[3]], w=[dst])

    def scaled_bf(srcf, tab, dst_bf):
        kb.op("dve", lambda e: e.tensor_tensor(out=dst_bf.t[:].rearrange("p (h d) -> p h d", d=128), in0=srcf.t[:],
                                               in1=tab.unsqueeze(2).broadcast_to([128, 8, 128]), op=ALU.mult), r=[srcf], w=[dst_bf])

    def state_contrib(kf, vt, tab_ap, tab_tile, acc):
        kw = nxt(stg, "stg")
        kb.op("dve", lambda e: e.tensor_tensor(out=kw.t[:].rearrange("p (h d) -> p h d", d=128), in0=kf.t[:],
                                               in1=tab_ap.unsqueeze(2).broadcast_to([128, 8, 128]), op=ALU.mult), r=[kf, tab_tile], w=[kw])
        for half in range(2):
            p = PB[4 + half]
            for hh in range(4):
                h = half * 4 + hh
                kb.op("pe", lambda e, h=h, hh=hh, p=p: e.matmul(p.t[:, hh * 128:(hh + 1) * 128], kw.t[:, h * 128:(h + 1) * 128],
                                                               vt.t[:, h * 128:(h + 1) * 128], start=True, stop=True),
                      r=[kw, vt], w=[p], inc=(hh == 3))
            kb.op("dve", lambda e, half=half, p=p: e.tensor_tensor(out=acc.t[:, half * 4:(half + 1) * 4, :], in0=acc.t[:, half * 4:(half + 1) * 4, :],
                                                                  in1=p.t[:].rearrange("p (h d) -> p h d", d=128), op=ALU.add), r=[acc, p], w=[acc])

    blocks = [tiles[b0:b0 + NB1] for b0 in range(0, NT1, NB1)]
    lnstate = {}

    def prep_ln(bi, ti):
        td_ = blocks[bi][ti]
        sl = (bi % 2) * NB1 + ti
        xt = nxt(xin, "x")
        hb_ = hb[(cnt["x"] - 1) % NHB]
        kb.dma("sp", xt.t[:], td_["x"], w=[xt])
        kb.dma("sp", ropet[sl].t[:], rope_d[td_["rope"], :, :, :], w=[ropet[sl]])
        if td_["kind"] == "own":
            layer_norm(xt, xt, hb_)
            kb.dma("pool", h_s[td_["own"], :, :], xt.t[:], r=[xt], w=[B_h[td_["own"]]])
        else:
            layer_norm(xt, None, hb_)
        lnstate[(bi, ti)] = hb_

    def prep_tr(bi, ti):
        sl = (bi % 2) * NB1 + ti
        hb_ = lnstate.pop((bi, ti))
        transpose_cols(hb_, lambda g0, n, sl=sl: hTs[sl].t[:, g0:g0 + n, :], KC, hTs[sl])

    def prep_stages(bi):
        if bi >= len(blocks):
            return []
        n = len(blocks[bi])
        acts = []
        for i in range(n):
            acts.append(("ln", i))
            if i >= NHB - 1:
                acts.append(("tr", i - (NHB - 1)))
        for i in range(max(0, n - (NHB - 1)), n):
            acts.append(("tr", i))
        k0 = min(n, NHB)
        st0 = [a for a in acts[:k0 + (1 if n >= NHB else 0)] if a[0] == "ln"][:k0]
        rest = [a for a in acts if a not in st0]
        st = [st0]
        if rest:
            st.append(rest[:-1])
            st.append(rest[-1:])
        return st

    def run_stage(bi, acts):
        for a, ti in acts:
            (prep_ln if a == "ln" else prep_tr)(bi, ti)

    sWc = kb.sb("sWc", [128, 8, 1 + NSL], F32)

    def state_contrib2(kf, vt, sl):
        kw = nxt(stg, "stg")
        kb.op("dve", lambda e: e.tensor_tensor(out=kw.t[:].rearrange("p (h d) -> p h d", d=128), in0=kf.t[:],
                                               in1=sWc.t[:, :, sl].unsqueeze(2).broadcast_to([128, 8, 128]), op=ALU.mult), r=[kf, sWc], w=[kw])
        for half in range(2):
            p = PB[4 + half]
            for hh in range(4):
                h = half * 4 + hh
                kb.op("pe", lambda e, h=h, hh=hh, p=p: e.matmul(p.t[:, hh * 128:(hh + 1) * 128], kw.t[:, h * 128:(h + 1) * 128],
                                                               vt.t[:, h * 128:(h + 1) * 128], start=True, stop=True),
                      r=[kw, vt], w=[p], inc=(hh == 3))
            for (acc, mcol) in ((SA[1], 1), (SA[2], 3)):
                kb.op("dve", lambda e, half=half, p=p, acc=acc, mcol=mcol: e.scalar_tensor_tensor(
                    out=acc.t[:, half * 4:(half + 1) * 4, :], in0=p.t[:].rearrange("p (h d) -> p h d", d=128), scalar=sic.t[:, mcol, sl:sl + 1],
                    in1=acc.t[:, half * 4:(half + 1) * 4, :], op0=ALU.mult, op1=ALU.add), r=[acc, p, sic], w=[acc])

    deferq = []

    def defer(fn):
        deferq.append(fn)

    def flush_defer():
        while deferq:
            deferq.pop(0)()

    pro = prep_stages(0)
    run_stage(0, pro.pop(0))
    kb.replay(late_consts)
    kb.op("pool", lambda e: e.tensor_tensor(out=sWc.t[:], in0=sWf.t[:], in1=sWb.t[:], op=ALU.add), r=[sWf, sWb], w=[sWc])
    for acts in pro:
        run_stage(0, acts)
    cast_in((("qa", 0), ("qr", 3072), ("gr", 6144)))
    for bi, blk in enumerate(blocks):
        units = [u for u in ("va", "ka", "vr", "kr", "qa", "qr", "gr") if any(u in UNITS[t["kind"]] for t in blk)]
        pending = prep_stages(bi + 1)
        per_unit = 1
        if pending:
            run_stage(bi + 1, pending.pop(0))
        for ui, u in enumerate(units):
            if ui > 0:
                for _ in range(per_unit):
                    if pending:
                        run_stage(bi + 1, pending.pop(0))
            ws = [nxt(wsl, "w"), nxt(wsl, "w")]
            for i2 in range(2):
                c0 = UCOL[u] + i2 * 512
                kb.dma("sp", ws[i2].t[:], wb_in[:, c0:c0 + 512].rearrange("(k p) c -> p k c", p=128), r=[WB_in[u]], w=[ws[i2]])
            for ti, td_ in enumerate(blk):
                if u not in UNITS[td_["kind"]]:
                    continue
                kind = td_["kind"]
                pp = [PB[(cnt["pb"] % 2) * 2], PB[(cnt["pb"] % 2) * 2 + 1]]
                cnt["pb"] += 1
                for i2 in range(2):
                    for k in range(KC):
                        kb.op("pe", lambda e, i2=i2, k=k, ti=ti: e.matmul(pp[i2].t[:], hTs[(bi % 2) * NB1 + ti].t[:, k, :], ws[i2].t[:, k, :],
                                                                          start=(k == 0), stop=(k == KC - 1)),
                              r=[hTs[(bi % 2) * NB1 + ti], ws[i2]], w=[pp[i2]], inc=(k == KC - 1))
                flush_defer()
                if u in ("va", "ka", "vr", "qa"):
                    dst = vtok[ti] if u == "vr" else nxt(stg, "stg")
                    sc = 0.125 if u == "qa" else 1.0
                    for i2 in range(2):
                        kb.op("act", lambda e, i2=i2, dst=dst, sc=sc: e.activation(out=dst.t[:, i2 * 512:(i2 + 1) * 512], in_=pp[i2].t[:], func=AF.Copy, scale=sc),
                              r=[pp[i2]], w=[dst])
                    if u == "va":
                        if kind == "meta":
                            kb.op("dve", lambda e, dst=dst: e.tensor_copy(out=va_meta.t[:, :, 0:64], in_=dst.t[0:16, :].rearrange("p (h d) -> p h d", d=64)),
                                  r=[dst], w=[va_meta])
                        else:
                            kb.dma("pool", va_s[td_["na"], :, :], dst.t[:], r=[dst], w=[B_va[td_["na"]]])
                    elif u == "ka":
                        if kind == "meta":
                            def _km(dst=dst):
                                tT = nxt(stgT, "stgT")
                                transpose_cols(dst, lambda g0, n, tT=tT: tT.t[:, g0:g0 + n, :], 8, tT)
                                kb.op("dve", lambda e, tT=tT: e.tensor_copy(out=kaT_meta.t[:], in_=tT.t[:, :, 0:16]), r=[tT], w=[kaT_meta])
                            defer(_km)
                        else:
                            defer(lambda dst=dst, td_=td_: store_T(dst, kaT_s[td_["na"], :, :, :], B_kaT[td_["na"]]))
                    elif u == "qa":
                        defer(lambda dst=dst, td_=td_: store_T(dst, qaT_s[td_["own"], :, :, :], B_qaT[td_["own"]]))
                    elif u == "vr" and kind == "own":
                        kb.dma("pool", vr_s[td_["own"], :, :], dst.t[:], r=[dst], w=[B_vr[td_["own"]]])
                elif u == "gr":
                    gs = nxt(gst, "gst")
                    for i2 in range(2):
                        kb.op("act", lambda e, i2=i2, gs=gs: e.activation(out=gs.t[:, i2 * 512:(i2 + 1) * 512], in_=pp[i2].t[:], func=AF.Silu), r=[pp[i2]], w=[gs])
                    kb.dma("pool", g_s[td_["own"], :, :], gs.t[:], r=[gs], w=[B_g[td_["own"]]])
                elif u == "kr":
                    kf = nxt(rof, "rof")
                    rope_unit(pp[0], pp[1], ropet[(bi % 2) * NB1 + ti], 2, kf)
                    if kind == "own":
                        o = td_["own"]
                        kbf = nxt(stg, "stg")
                        kb.op("act", lambda e, kbf=kbf, kf=kf: e.activation(out=kbf.t[:], in_=kf.t[:].rearrange("p h d -> p (h d)"), func=AF.Copy), r=[kf], w=[kbf])
                        defer(lambda kbf=kbf, o=o: store_T(kbf, krT_s[o, :, :, :], B_krT[o]))
                        for (tab, dsc, bb) in ((zef, kzf_s, B_kzf), (zeb, kzb_s, B_kzb)):
                            kz = nxt(stg, "stg")
                            kb.op("dve", lambda e, kz=kz, tab=tab, kf=kf: e.tensor_tensor(out=kz.t[:].rearrange("p (h d) -> p h d", d=128), in0=kf.t[:],
                                                                                       in1=tab.t[:].unsqueeze(2).broadcast_to([128, 8, 128]), op=ALU.mult),
                                  r=[kf, tab], w=[kz])
                            kb.dma("pool", dsc[o, :, :], kz.t[:], r=[kz], w=[bb[o]])
                    elif kind == "meta":
                        defer(lambda kf=kf, ti=ti: state_contrib(kf, vtok[ti], sPf.t[:, :], sPf, SA[0]))
                        defer(lambda kf=kf, ti=ti: state_contrib(kf, vtok[ti], sWf.t[:, :, 0], sWf, SA[1]))
                    else:
                        sl = td_["slot"]
                        defer(lambda kf=kf, ti=ti, sl=sl: state_contrib2(kf, vtok[ti], sl))
                elif u == "qr":
                    o = td_["own"]
                    qf = nxt(rof, "rof")
                    rope_unit(pp[0], pp[1], ropet[(bi % 2) * NB1 + ti], 0, qf)
                    q0 = nxt(stg, "stg")
                    kb.op("act", lambda e, q0=q0, qf=qf: e.activation(out=q0.t[:], in_=qf.t[:].rearrange("p h d -> p (h d)"), func=AF.Copy), r=[qf], w=[q0])
                    defer(lambda q0=q0, o=o: store_T(q0, qrT_s[o, 0, :, :, :], B_qrT[o][0]))
                    for vi, tab in ((1, xif), (2, xib)):
                        qv = nxt(stg, "stg")
                        kb.op("dve", lambda e, qv=qv, tab=tab, qf=qf: e.tensor_tensor(out=qv.t[:].rearrange("p (h d) -> p h d", d=128), in0=qf.t[:],
                                                                                   in1=tab.t[:].unsqueeze(2).broadcast_to([128, 8, 128]), op=ALU.mult),
                              r=[qf, tab], w=[qv])
                        def _qv(qv=qv, o=o, vi=vi):
                            tT = nxt(stgT, "stgT")
                            transpose_cols(qv, lambda g0, n, tT=tT: tT.t[:, g0:g0 + n, :], 8, tT)
                            kb.dma("pool", qrT_s[o, vi, :, :, :], tT.t[:], r=[tT], w=[B_qrT[o][vi]])
                        defer(_qv)
        flush_defer()
        while pending:
            run_stage(bi + 1, pending.pop(0))
        if bi >= 2 and bi % 2 == 0 and cast_jobs:
            cast_jobs.pop(0)()
    for i in range(3):
        kb.dma("pool", sinit_s[i, :, :], SA[i].t[:].rearrange("p h d -> p (h d)"), r=[SA[i]])


    if getattr(cfg, "stop", 99) == 1:
        kb.finish()
        kb.close_all()
        return kb
    kb.pop()
    kb.push()
    while cast_jobs:
        cast_jobs.pop(0)()

    Sst = kb.sb("Sst", [128, 8, 128], F32)
    Sbf = [kb.sb("Sbf%d" % i, [128, 8, 128], BF16) for i in range(2)]
    ldk = [kb.sb("ldk%d" % i, [128, 1024], BF16) for i in range(2)]
    ldv = [kb.sb("ldv%d" % i, [128, 1024], BF16) for i in range(2)]
    cnt.update(sbf=0, ld=0)

    def state_step(S, kz_t, v_t, gc):
        for half in range(2):
            p = PB[4 + half]
            for hh in range(4):
                h = half * 4 + hh
                kb.op("pe", lambda e, h=h, hh=hh, p=p: e.matmul(p.t[:, hh * 128:(hh + 1) * 128], kz_t.t[:, h * 128:(h + 1) * 128],
                                                               v_t.t[:, h * 128:(h + 1) * 128], start=True, stop=True),
                      r=[kz_t, v_t], w=[p], inc=(hh == 3))
            for hh in range(4):
                h = half * 4 + hh
                kb.op("dve", lambda e, h=h, hh=hh, p=p: e.scalar_tensor_tensor(out=S.t[:, h, :], in0=S.t[:, h, :], scalar=gc.t[:, h:h + 1],
                                                                              in1=p.t[:, hh * 128:(hh + 1) * 128], op0=ALU.mult, op1=ALU.add),
                      r=[S, gc, p], w=[S])

    seqs = [(0, NT_P, None), (NT_P, NOWN, SA[2])]
    Gb = kb.sb("Gb", [128, 8, 128], F32)
    kb.op("dve", lambda e: e.tensor_copy(out=Gb.t[:], in_=gcb.t[:].unsqueeze(2).broadcast_to([128, 8, 128])), r=[gcb], w=[Gb])
    Spp = [Sst, kb.sb("Sst2", [128, 8, 128], F32)]
    ldk = ldk + [kb.sb("ldk%d" % i, [128, 1024], BF16) for i in range(2, 4)]
    ldv = ldv + [kb.sb("ldv%d" % i, [128, 1024], BF16) for i in range(2, 4)]
    Sbf = Sbf + [kb.sb("Sbf%d" % i, [128, 8, 128], BF16) for i in range(2, 4)]
    cnt.update(pp2=0, spp=0)

    def state_step2(S_in, S_out, kz_t, v_t):
        ps_ = []
        for half in range(2):
            p = PB[cnt["pp2"] % 6]
            cnt["pp2"] += 1
            for hh in range(4):
                h = half * 4 + hh
                kb.op("pe", lambda e, h=h, hh=hh, p=p: e.matmul(p.t[:, hh * 128:(hh + 1) * 128], kz_t.t[:, h * 128:(h + 1) * 128],
                                                               v_t.t[:, h * 128:(h + 1) * 128], start=True, stop=True),
                      r=[kz_t, v_t], w=[p], inc=(hh == 3))
            ps_.append(p)
        kb.op("dve", lambda e: e.tensor_tensor(out=S_out.t[:], in0=S_in.t[:], in1=Gb.t[:], op=ALU.mult), r=[S_in, Gb], w=[S_out])
        for half in range(2):
            p = ps_[half]
            kb.op("dve", lambda e, half=half, p=p: e.tensor_tensor(out=S_out.t[:, half * 4:(half + 1) * 4, :], in0=S_out.t[:, half * 4:(half + 1) * 4, :],
                                                                  in1=p.t[:].rearrange("p (h d) -> p h d", d=128), op=ALU.add), r=[S_out, p], w=[S_out])

    for (o0, o1, init) in seqs:
        S_cur = Spp[cnt["spp"] % 2]
        if init is None:
            kb.op("pool", lambda e, S_cur=S_cur: e.memset(S_cur.t[:], 0.0), w=[S_cur])
        else:
            kb.op("pool", lambda e, init=init, S_cur=S_cur: e.tensor_copy(out=S_cur.t[:], in_=init.t[:]), r=[init], w=[S_cur])
        for o in range(o1 - 1, o0 - 1, -1):
            sbf = nxt(Sbf, "sbf")
            kb.op("act", lambda e, sbf=sbf, S_cur=S_cur: e.activation(out=sbf.t[:], in_=S_cur.t[:], func=AF.Copy), r=[S_cur], w=[sbf])
            kb.dma("pool", sb_s[o, :, :, :], sbf.t[:], r=[sbf], w=[B_sb[o]])
            if o > o0:
                kz_t, v_t = ldk[cnt["ld"] % 4], ldv[cnt["ld"] % 4]
                cnt["ld"] += 1
                kb.dma("sp", kz_t.t[:], kzb_s[o, :, :], r=[B_kzb[o]], w=[kz_t])
                kb.dma("sp", v_t.t[:], vr_s[o, :, :], r=[B_vr[o]], w=[v_t])
                cnt["spp"] += 1
                S_nxt = Spp[cnt["spp"] % 2]
                state_step2(S_cur, S_nxt, kz_t, v_t)
                S_cur = S_nxt

    if getattr(cfg, "stop", 99) == 2:
        kb.finish()
        kb.close_all()
        return kb
    kb.pop()
    kb.push()
    stgT = [kb.sb("stgT%d" % i, [128, 8, 128], BF16) for i in range(3)]
    gst = [kb.sb("gst%d" % i, [128, 1024], F32) for i in range(1)]
    biasu = kb.sb("biasu", [128, 7, 16, 128], BF16)
    maskp = kb.sb("maskp", [128, NVP, 128], BF16)
    masks = kb.sb("masks", [128, NS_OWN * 7, 128], BF16)
    for dlt in range(7):
        for hq in range(4):
            tmpf = nxt(gst, "gst")
            kb.dma("sp", tmpf.t[:, 0:512], biasu_d[:, dlt, hq * 4:(hq + 1) * 4, :].rearrange("p h q -> p (h q)"), w=[tmpf])
            kb.op("dve", lambda e, dlt=dlt, hq=hq, tmpf=tmpf: e.tensor_copy(out=biasu.t[:, dlt, hq * 4:(hq + 1) * 4, :].rearrange("p h q -> p (h q)"),
                                                                          in_=tmpf.t[:, 0:512]), r=[tmpf], w=[biasu])
    for (src_d, dst_t, nv) in ((maskp_d, maskp, NVP), (masks_d, masks, NS_OWN * 7)):
        for v0 in range(0, nv, 8):
            n = min(8, nv - v0)
            tmpf = nxt(gst, "gst")
            kb.dma("sp", tmpf.t[:, 0:n * 128], src_d[:, v0:v0 + n, :].rearrange("p v q -> p (v q)"), w=[tmpf])
            kb.op("dve", lambda e, v0=v0, n=n, tmpf=tmpf, dst_t=dst_t: e.tensor_copy(out=dst_t.t[:, v0:v0 + n, :].rearrange("p v q -> p (v q)"),
                                                                                  in_=tmpf.t[:, 0:n * 128]), r=[tmpf], w=[dst_t])
    gng = kb.sb("gng", [128, 1024], F32)
    kb.dma("sp", gng.t[:], gng_d[:, :], w=[gng])

    NRK = 8
    rk = [kb.sb("rk%d" % i, [128, 8, 128], BF16) for i in range(NRK)]
    rv = [kb.sb("rv%d" % i, [128, 16, 66], BF16) for i in range(NRK)]
    for t_ in rv:
        kb.op("pool", lambda e, t_=t_: e.memset(t_.t[:], 1.0), w=[t_])
    ring_of = {}
    cnt.update(ring=0, q=0, pt=0, nb=0)

    def get_key(na):
        if na in ring_of and ring_of[na][2] == na:
            return ring_of[na][0], ring_of[na][1]
        i = cnt["ring"] % NRK
        cnt["ring"] += 1
        for k_ in list(ring_of):
            if ring_of[k_][3] == i:
                del ring_of[k_]
        kb.dma("sp", rk[i].t[:], kaT_s[na, :, :, :], r=[B_kaT[na]], w=[rk[i]])
        kb.dma("sp", rv[i].t[:, :, 0:64], va_s[na, :, :].rearrange("p (h d) -> p h d", d=64), r=[B_va[na]], w=[rv[i]])
        ring_of[na] = (rk[i], rv[i], na, i)
        return rk[i], rv[i]

    qa_t = [kb.sb("qa_t%d" % i, [128, 8, 128], BF16) for i in range(2)]
    qo_t = [kb.sb("qo_t%d" % i, [128, 8, 128], BF16) for i in range(2)]
    for t_ in qa_t + qo_t:
        kb.op("pool", lambda e, t_=t_: e.memset(t_.t[:], 0.0), w=[t_])
    qr_t = [kb.sb("qr_t%d" % i, [128, 3, 8, 128], BF16) for i in range(1)]
    g_t = [kb.sb("g_t%d" % i, [128, 1024], F32) for i in range(1)]
    kr_t = [kb.sb("kr_t%d" % i, [128, 8, 128], BF16) for i in range(1)]
    kzf_t = [kb.sb("kzf_t%d" % i, [128, 1024], BF16) for i in range(1)]
    vr_t = [kb.sb("vr_t%d" % i, [128, 1024], BF16) for i in range(1)]
    sb_t = [kb.sb("sb_t%d" % i, [128, 8, 128], BF16) for i in range(1)]
    PTs = [kb.sb("PTs%d" % i, [128, 4, 128], BF16) for i in range(17)]
    PTm = [kb.sb("PTm%d" % i, [16, 4, 128], BF16) for i in range(2)]
    mix = [kb.sb("mix%d" % i, [128, D], BF16) for i in range(2)]
    rec = kb.sb("rec", [128, 4], F32)
    orf = kb.sb("orf", [128, 8, 128], F32)
    gst6 = kb.sb("gst6", [128, 8, 6], F32)
    gmv = kb.sb("gmv", [128, 8, 2], F32)
    grs = kb.sb("grs", [128, 8], F32)
    Sf = kb.sb("Sf", [128, 8, 128], F32)
    Sfb = kb.sb("Sfb", [128, 8, 128], BF16)

    class _Stop(Exception):
        pass

    def chk(n):
        if getattr(cfg, "stop", 99) == n:
            raise _Stop()

    def p3_body():
      tail_q = []
      for si, (o0, o1, _) in enumerate(seqs):
          kb.op("pool", lambda e, si=si: e.tensor_copy(out=Sf.t[:], in_=SA[si].t[:]), r=[SA[si]], w=[Sf])
          kb.op("act", lambda e: e.activation(out=Sfb.t[:], in_=Sf.t[:], func=AF.Copy), r=[Sf], w=[Sfb])
          ring_of.clear()
          for o in range(o0, o1):
              j = o - o0
              qi = 0
              cnt["q"] += 1
              q2 = cnt["q"] % 2
              qa, qr, gg, kr, kzf, vr, sbt, mx = qa_t[q2], qr_t[qi], g_t[qi], kr_t[qi], kzf_t[qi], vr_t[qi], sb_t[qi], mix[q2]
              qeo = (qa, qo_t[q2])
              kb.dma("sp", qa.t[0:64, :, :], qaT_s[o, 0:64, :, :], r=[B_qaT[o]], w=[qa])
              kb.dma("sp", qeo[1].t[64:128, :, :], qaT_s[o, 64:128, :, :], r=[B_qaT[o]], w=[qeo[1]])
              if si == 0:
                  dl = [(d_, cfg.pvar[(j, d_)]) for d_ in range(-3, 4) if (j, d_) in cfg.pvar]
                  keys = [(d_, get_key(j + d_), maskp.t[:, v, :], maskp) for d_, v in dl]
              else:
                  keys = [(d_, get_key(NT_P + 3 + j + d_), masks.t[:, j * 7 + d_ + 3, :], masks) for d_ in range(-3, 4)]
              kb.dma("sp", qr.t[:], qrT_s[o, :, :, :, :].rearrange("v p c t -> p v c t"), r=B_qrT[o], w=[qr])
              kb.dma("sp", kr.t[:], krT_s[o, :, :, :], r=[B_krT[o]], w=[kr])
              kb.dma("sp", kzf.t[:], kzf_s[o, :, :], r=[B_kzf[o]], w=[kzf])
              kb.dma("sp", vr.t[:], vr_s[o, :, :], r=[B_vr[o]], w=[vr])
              kb.dma("sp", sbt.t[:], sb_s[o, :, :, :], r=[B_sb[o]], w=[sbt])
              kb.dma("sp", gg.t[:], g_s[o, :, :], r=[B_g[o]], w=[gg])

              def na_scores(hg):
                  pts = []
                  for (d_, (rk_, rv_), mk, mkt) in keys:
                      p = PB[cnt["nb"] % 4]
                      cnt["nb"] += 1
                      for hh in range(4):
                          h = hg * 4 + hh
                          c, qq = h // 2, qeo[h % 2]
                          kb.op("pe", lambda e, hh=hh, c=c, qq=qq, rk_=rk_, p=p: e.matmul(p.t[:, hh * 128:(hh + 1) * 128], rk_.t[:, c, :],
                                                                                         qq.t[:, c, :], start=(hh == 0), stop=False,
                                                                                         skip_group_check=True),
                                r=[rk_, qq], w=[p], inc=False)
                      kb.op("pe", lambda e, d_=d_, p=p: e.matmul(p.t[:], ident.t[:], biasu.t[:, d_ + 3, hg * 4:(hg + 1) * 4, :].rearrange("p h q -> p (h q)"),
                                                                start=False, stop=False, skip_group_check=True), r=[ident, biasu], w=[p], inc=False)
                      for hh in range(4):
                          kb.op("pe", lambda e, mk=mk, p=p, hh=hh: e.matmul(p.t[:, hh * 128:(hh + 1) * 128], ident.t[:], mk, start=False, stop=True,
                                                                           skip_group_check=True),
                                r=[ident, mkt], w=[p], inc=(hh == 3))
                      pt_ = nxt(PTs, "pt")
                      kb.op("act", lambda e, pt_=pt_, p=p: e.activation(out=pt_.t[:].rearrange("p h q -> p (h q)"), in_=p.t[:], func=AF.Exp), r=[p], w=[pt_])
                      pts.append((pt_, rv_))
                  p = PB[cnt["nb"] % 4]
                  cnt["nb"] += 1
                  for hh in range(4):
                      h = hg * 4 + hh
                      c, qq = h // 2, qeo[h % 2]
                      kb.op("pe", lambda e, hh=hh, c=c, qq=qq, p=p: e.matmul(p.t[0:16, hh * 128:(hh + 1) * 128], kaT_meta.t[:, c, :],
                                                                            qq.t[:, c, :], start=True, stop=True),
                            r=[kaT_meta, qq], w=[p], inc=(hh == 3))
                  ptm = PTm[hg % 2]
                  kb.op("act", lambda e, ptm=ptm, p=p: e.activation(out=ptm.t[:].rearrange("p h q -> p (h q)"), in_=p.t[0:16, :], func=AF.Exp), r=[p], w=[ptm])
                  return pts, ptm

              def na_pv(hg, pts, ptm):
                  po_ = PB[4 + (hg % 2)]
                  for hh in range(4):
                      h = hg * 4 + hh
                      for ki, (pt_, rv_) in enumerate(pts):
                          kb.op("pe", lambda e, hh=hh, h=h, pt_=pt_, rv_=rv_, ki=ki: e.matmul(po_.t[:, hh * 65:(hh + 1) * 65], pt_.t[:, hh, :], rv_.t[:, h, 0:65],
                                                                                           start=(ki == 0), stop=False),
                                r=[pt_, rv_], w=[po_], inc=False)
                      kb.op("pe", lambda e, hh=hh, h=h: e.matmul(po_.t[:, hh * 65:(hh + 1) * 65], ptm.t[:, hh, :], va_meta.t[:, h, 0:65], start=False, stop=True),
                            r=[ptm, va_meta], w=[po_], inc=(hh == 3))
                  pov = po_.t[:, 0:260].rearrange("p (h d) -> p h d", d=65)
                  kb.op("dve", lambda e: e.reciprocal(out=rec.t[:].unsqueeze(2), in_=pov[:, :, 64:65]), r=[po_], w=[rec])
                  kb.op("dve", lambda e, hg=hg: e.tensor_tensor(out=mx.t[:, hg * 256:(hg + 1) * 256].rearrange("p (h d) -> p h d", d=64), in0=pov[:, :, 0:64],
                                                                in1=rec.t[:].unsqueeze(2).broadcast_to([128, 4, 64]), op=ALU.mult), r=[po_, rec], w=[mx])

              ret_pts = []

              def ret_scores():
                  for half in range(2):
                      p = PB[cnt["nb"] % 4]
                      cnt["nb"] += 1
                      for hh in range(4):
                          h = half * 4 + hh
                          kb.op("pe", lambda e, hh=hh, h=h, p=p: e.matmul(p.t[:, hh * 128:(hh + 1) * 128], kr.t[:, h, :], qr.t[:, 0, h, :], start=True, stop=True),
                                r=[kr, qr], w=[p], inc=(hh == 3))
                      pt_ = nxt(PTs, "pt")
                      kb.op("dve", lambda e, pt_=pt_, p=p, half=half: e.tensor_tensor(out=pt_.t[:], in0=p.t[:].rearrange("p (h q) -> p h q", q=128),
                                                                                    in1=AT.t[:, half * 4:(half + 1) * 4, :], op=ALU.mult), r=[p, AT], w=[pt_])
                      ret_pts.append(pt_)

              sc = na_scores(0)
              while tail_q:
                  tail_q.pop(0)()
              for hg in range(4):
                  nsc = na_scores(hg + 1) if hg < 3 else None
                  na_pv(hg, *sc)
                  sc = nsc
                  if hg == 1:
                      ret_scores()
              for half in range(2):
                  pt_ = ret_pts[half]
                  po_ = PB[4 + half]
                  for hh in range(4):
                      h = half * 4 + hh
                      oo = po_.t[:, hh * 128:(hh + 1) * 128]
                      kb.op("pe", lambda e, oo=oo, hh=hh, h=h, pt_=pt_: e.matmul(oo, pt_.t[:, hh, :], vr.t[:, h * 128:(h + 1) * 128], start=True, stop=False),
                            r=[pt_, vr], w=[po_], inc=False)
                      kb.op("pe", lambda e, oo=oo, h=h: e.matmul(oo, qr.t[:, 1, h, :], Sfb.t[:, h, :], start=False, stop=False), r=[qr, Sfb], w=[po_], inc=False)
                      kb.op("pe", lambda e, oo=oo, h=h: e.matmul(oo, qr.t[:, 2, h, :], sbt.t[:, h, :], start=False, stop=True), r=[qr, sbt], w=[po_], inc=(hh == 3))
                  kb.op("act", lambda e, half=half, po_=po_: e.activation(out=orf.t[:, half * 4:(half + 1) * 4, :].rearrange("p h d -> p (h d)"), in_=po_.t[:], func=AF.Copy),
                        r=[po_], w=[orf])
              state_step(Sf, kzf, vr, gcf)
              kb.op("act", lambda e: e.activation(out=Sfb.t[:], in_=Sf.t[:], func=AF.Copy), r=[Sf], w=[Sfb])
              for h in range(8):
                  kb.op("dve", lambda e, h=h: e.bn_stats(out=gst6.t[:, h, :], in_=orf.t[:, h, :]), r=[orf], w=[gst6])
              for h in range(8):
                  kb.op("dve", lambda e, h=h: e.bn_aggr(out=gmv.t[:, h, :], in_=gst6.t[:, h, :]), r=[gst6], w=[gmv])
              kb.op("act", lambda e: e.activation(out=grs.t[:].unsqueeze(2), in_=gmv.t[:, :, 1:2], func=AF.Sqrt, bias=EPS, scale=1.0), r=[gmv], w=[grs])
              kb.op("dve", lambda e: e.reciprocal(out=grs.t[:], in_=grs.t[:]), r=[grs], w=[grs])
              kb.op("dve", lambda e: e.tensor_tensor(out=orf.t[:], in0=orf.t[:], in1=gmv.t[:, :, 0:1].broadcast_to([128, 8, 128]), op=ALU.subtract), r=[orf, gmv], w=[orf])
              kb.op("dve", lambda e: e.tensor_tensor(out=orf.t[:], in0=orf.t[:], in1=grs.t[:].unsqueeze(2).broadcast_to([128, 8, 128]), op=ALU.mult), r=[orf, grs], w=[orf])
              orf2 = orf.t[:].rearrange("p h d -> p (h d)")
              kb.op("pool", lambda e, orf2=orf2: e.tensor_tensor(out=orf2, in0=orf2, in1=gng.t[:], op=ALU.mult), r=[orf, gng], w=[orf])
              kb.op("dve", lambda e, orf2=orf2, mx=mx, gg=gg: e.tensor_tensor(out=mx.t[:, 1024:2048], in0=orf2, in1=gg.t[:], op=ALU.mult), r=[orf, gg], w=[mx])

              def tail(mx=mx, o=o):
                  tTa, tTb = nxt(stgT, "stgT"), nxt(stgT, "stgT")
                  transpose_cols(mx, lambda g0, n, tTa=tTa: tTa.t[:, g0:g0 + n, :], 8, tTa)
                  transpose_cols(mx, lambda g0, n, tTb=tTb: tTb.t[:, g0:g0 + n, :], 8, tTb, src_off=1024)
                  bmA, bmB = Buf("mA"), Buf("mB")
                  kb.dma("pool", mixT_s[o, :, 0:8, :], tTa.t[:], r=[tTa], w=[bmA])
                  kb.dma("pool", mixT_s[o, :, 8:16, :], tTb.t[:], r=[tTb], w=[bmB])
                  B_mixT[o] = [bmA, bmB]
              tail_q.append(tail)
      while tail_q:
          tail_q.pop(0)()

    try:
        chk(30)
        p3_body()
    except _Stop:
        kb.finish()
        kb.close_all()
        return kb
    if getattr(cfg, "stop", 99) == 3:
        kb.finish()
        kb.close_all()
        return kb
    kb.pop()
    kb.pop()
    kb.push()
    NB4 = 4
    load_ln(1)
    stgT = [kb.sb("stgT%d" % i, [128, 8, 128], BF16) for i in range(3)]
    hf = [kb.sb("hf%d" % i, [128, D], F32) for i in range(2)]
    hb = [kb.sb("hb%d" % i, [128, D], BF16) for i in range(2)]
    wout = kb.sb("wout", [128, KC, D], BF16)
    kb.dma("sp", wout.t[:], wb_out[:, :].rearrange("(k p) c -> p k c", p=128), r=WB_out, w=[wout])
    aT1 = [kb.sb("aT1_%d" % i, [128, KC, 128], BF16) for i in range(2)]
    resid = [kb.sb("resid%d" % i, [128, D], F32) for i in range(2)]
    ybuf = [kb.sb("ybuf%d" % i, [128, D], F32) for i in range(2)]

    def p4_mm(o):
        s2 = o % 2
        mA, mB = B_mixT[o]
        kb.dma("sp", aT1[s2].t[:], mixT_s[o, :, :, :], r=[mA, mB], w=[aT1[s2]])
        kb.dma("sp", resid[s2].t[:], h_s[o, :, :], r=[B_h[o]], w=[resid[s2]])
        for cb in range(4):
            p = PB[cnt["nb"] % 4]
            cnt["nb"] += 1
            for k in range(KC):
                kb.op("pe", lambda e, k=k, p=p: e.matmul(p.t[:], aT1[s2].t[:, k, :], wout.t[:, k, cb * 512:(cb + 1) * 512], start=(k == 0), stop=(k == KC - 1)),
                      r=[aT1[s2], wout], w=[p], inc=(k == KC - 1))
            kb.op("dve", lambda e, p=p: e.scalar_tensor_tensor(out=ybuf[s2].t[:, cb * 512:(cb + 1) * 512], in0=resid[s2].t[:, cb * 512:(cb + 1) * 512],
                                                              scalar=ALPHA, in1=p.t[:], op0=ALU.mult, op1=ALU.add), r=[resid[s2], p], w=[ybuf[s2]])
        layer_norm(ybuf[s2], hf[s2], hb[s2])
        kb.dma("pool", h1_s[o, :, :], hf[s2].t[:], r=[hf[s2]], w=[B_h1[o]])

    def p4_tr(o):
        hb_ = hb[o % 2]
        tTa, tTb = nxt(stgT, "stgT"), nxt(stgT, "stgT")
        transpose_cols(hb_, lambda g0, n, tTa=tTa: tTa.t[:, g0:g0 + n, :], 8, tTa)
        transpose_cols(hb_, lambda g0, n, tTb=tTb: tTb.t[:, g0:g0 + n, :], 8, tTb, src_off=1024)
        bmA, bmB = Buf("hA"), Buf("hB")
        kb.dma("pool", h1T_s[o, :, 0:8, :], tTa.t[:], r=[tTa], w=[bmA])
        kb.dma("pool", h1T_s[o, :, 8:16, :], tTb.t[:], r=[tTb], w=[bmB])
        B_h1T[o] = [bmA, bmB]

    for i in range(NOWN + 1):
        if i < NOWN:
            p4_mm(i)
        if i >= 1:
            p4_tr(i - 1)

    kb.pop()
    kb.push()
    load_ln(2)
    aTin = kb.sb("aTin", [128, KC, NB4 * 128], BF16)
    resid1 = kb.sb("resid1", [128, D], F32)
    ybuf = [kb.sb("ybuf%d" % i, [128, D], F32) for i in range(NB4)]
    wsl = [kb.sb("wgu%d" % i, [128, KC, 256], BF16) for i in range(3)]
    aT = kb.sb("aT", [128, FC, NB4 * 128], BF16)
    sg = [kb.sb("sg%d" % i, [128, 512], F32) for i in range(2)]
    wdn = [kb.sb("wdn%d" % i, [128, FC, 256], BF16) for i in range(2)]
    cnt.update(sg=0, wd=0)
    NG = 2
    aTb = bufs("aTinb", NB4)
    blocks5 = [list(range(b0, min(NOWN, b0 + NB4))) for b0 in range(0, NOWN, NB4)]

    def load_aTin(own, q):
        for ti, o in enumerate(own):
            kb.dma(q, aTin.t[:, :, ti * 128:(ti + 1) * 128], h1T_s[o, :, :, :], r=B_h1T[o], w=[aTb[ti]])

    def ln2_tile(o, ti):
        kb.dma("sp", resid1.t[:], h1_s[o, :, :], r=[B_h1[o]], w=[resid1])
        kb.op("dve", lambda e, ti=ti: e.scalar_tensor_tensor(out=ybuf[ti].t[:], in0=resid1.t[:], scalar=ALPHA, in1=ybuf[ti].t[:], op0=ALU.mult, op1=ALU.add),
              r=[resid1, ybuf[ti]], w=[ybuf[ti]])
        layer_norm(ybuf[ti], ybuf[ti], None)
        kb.dma("pool", y_out[o * 128:(o + 1) * 128, :], ybuf[ti].t[:], r=[ybuf[ti]])

    pend = []
    load_aTin(blocks5[0], "sp")
    for bi5, own in enumerate(blocks5):
        nt = len(own) * 128
        ab = aTb[:len(own)]
        for gi, f0 in enumerate(range(0, FC, NG)):
            ng = min(NG, FC - f0)
            wg_, wu_ = nxt(wsl, "w"), nxt(wsl, "w")
            kb.dma("sp", wg_.t[:, :, 0:ng * 128], wb_g[:, f0 * 128:(f0 + ng) * 128].rearrange("(k p) c -> p k c", p=128), r=WB_g, w=[wg_])
            kb.dma("sp", wu_.t[:, :, 0:ng * 128], wb_u[:, f0 * 128:(f0 + ng) * 128].rearrange("(k p) c -> p k c", p=128), r=WB_u, w=[wu_])
            for fl in range(ng):
                fc = f0 + fl
                pg, pu = PB[(cnt["nb"] % 2) * 2], PB[(cnt["nb"] % 2) * 2 + 1]
                cnt["nb"] += 1
                for (pw, wt) in ((pg, wg_), (pu, wu_)):
                    for k in range(KC):
                        kb.op("pe", lambda e, k=k, pw=pw, wt=wt, fl=fl: e.matmul(pw.t[:, 0:nt], wt.t[:, k, fl * 128:(fl + 1) * 128], aTin.t[:, k, 0:nt],
                                                                               start=(k == 0), stop=(k == KC - 1)),
                              r=[wt] + ab, w=[pw], inc=(k == KC - 1))
                sg_ = nxt(sg, "sg")
                kb.op("act", lambda e, sg_=sg_, pg=pg: e.activation(out=sg_.t[:, 0:nt], in_=pg.t[:, 0:nt], func=AF.Silu), r=[pg], w=[sg_])
                kb.op("dve", lambda e, sg_=sg_, pu=pu, fc=fc: e.tensor_tensor(out=aT.t[:, fc, 0:nt], in0=sg_.t[:, 0:nt], in1=pu.t[:, 0:nt], op=ALU.mult),
                      r=[sg_, pu], w=[aT])
            if pend and gi >= 1 and gi % 3 == 1:
                ln2_tile(*pend.pop(0))
        while pend:
            ln2_tile(*pend.pop(0))
        if bi5 + 1 < len(blocks5):
            load_aTin(blocks5[bi5 + 1], "pool")
        for cb in range(8):
            wd_ = nxt(wdn, "wd")
            kb.dma("sp", wd_.t[:], wb_d[:, cb * 256:(cb + 1) * 256].rearrange("(k p) c -> p k c", p=128), r=WB_d, w=[wd_])
            for ti, o in enumerate(own):
                p = PB[4 + cnt["nb"] % 2]
                cnt["nb"] += 1
                for fc in range(FC):
                    kb.op("pe", lambda e, fc=fc, ti=ti, p=p: e.matmul(p.t[:, 0:256], aT.t[:, fc, ti * 128:(ti + 1) * 128], wd_.t[:, fc, :], start=(fc == 0), stop=(fc == FC - 1)),
                          r=[aT, wd_], w=[p], inc=(fc == FC - 1))
                kb.op("act", lambda e, ti=ti, cb=cb, p=p: e.activation(out=ybuf[ti].t[:, cb * 256:(cb + 1) * 256], in_=p.t[:, 0:256], func=AF.Copy), r=[p], w=[ybuf[ti]])
        pend = [(o, ti) for ti, o in enumerate(own)]
    while pend:
        ln2_tile(*pend.pop(0))
    kb.finish()
    kb.close_all()
    return kb


def na_mask(R, j, dlt):
    kt = j + dlt
    if kt < 0 or kt >= R // 2:
        return None
    m = np.full((128, 128), NEG, np.float32)
    kr = np.arange(128) // 64 + 2 * kt
    kc = np.arange(128) % 64
    for q in range(128):
        qrow, qc = 2 * j + q // 64, q % 64
        rs = min(max(qrow - 4, 0), R - 8)
        cs = min(max(qc - 8, 0), 64 - 16)
        ok = (kr >= rs) & (kr < rs + 8) & (kc >= cs) & (kc < cs + 16)
        m[ok, q] = 0.0
    return m


def prompt_masks(NT_P):
    R = 2 * NT_P
    var, tabs, keyd = {}, [], {}
    for j in range(NT_P):
        for dlt in range(-3, 4):
            m = na_mask(R, j, dlt)
            if m is None or not (m == 0).any():
                continue
            kk = m.tobytes()
            if kk not in keyd:
                keyd[kk] = len(tabs)
                tabs.append(m)
            var[(j, dlt)] = keyd[kk]
    return var, np.stack(tabs, 1)


def host_inputs(cfg, core, inp):
    NT_P, NT_S, NS_OWN, NWIN, NSL = cfg.NT_P, cfg.NT_S, cfg.NS_OWN, cfg.NWIN, cfg.NSL
    f32 = np.float32
    xsamp = inp["x_sample"][0]
    m = {}
    xm = np.zeros((128, D), f32)
    xm[:16] = inp["meta_tokens"]
    m["xm"] = xm
    m["xp"] = np.ascontiguousarray(inp["x_prompt"][core])
    g0 = NS_OWN * core - 3
    slots = [g if 0 <= g < NT_S else -1 for g in range(g0, g0 + NWIN)]
    rest = [g for g in range(NT_S) if g not in slots]
    slots += rest + [-1] * (NSL - NWIN - len(rest))
    xs = np.zeros((NSL * 128, D), f32)
    for s, g in enumerate(slots):
        if g >= 0:
            xs[s * 128:(s + 1) * 128] = xsamp[g * 128:(g + 1) * 128]
    m["xs"] = xs
    m["w_in"], m["w_out"] = inp["w_in"][0], inp["w_out"][0]
    m["w_g"], m["w_u"], m["w_d"] = inp["w_ffn_gate"][0], inp["w_ffn_up"][0], inp["w_ffn_down"][0]
    rep = lambda v, n=128: np.ascontiguousarray(np.broadcast_to(np.asarray(v, f32).reshape(1, -1), (n, np.asarray(v).size)))
    m["lnt"] = np.stack([rep(inp["ln_in_g"]), rep(inp["ln_in_b"]), rep(inp["ln1_g"][0]), rep(inp["ln1_b"][0]),
                         rep(inp["ln2_g"][0]), rep(inp["ln2_b"][0])], 0)
    m["gng"] = rep(inp["ret_gn_g"][0])
    m["dec"] = rep(np.concatenate([inp["ret_decay_f"][0], inp["ret_decay_b"][0]]))
    m["ident"] = np.eye(128, dtype=f32)
    half = 64
    inv = (f32(10000.0) ** (-np.arange(half, dtype=f32) / f32(half))).astype(f32)
    pos = np.zeros((cfg.NT1, 128), f32)
    pos[0, :16] = np.arange(16)
    for j in range(NT_P):
        pos[1 + j] = 16 + 128 * j + np.arange(128)
    for s, g in enumerate(slots):
        if g >= 0:
            pos[1 + NT_P + s] = 16 + 128 * g + np.arange(128)
    ang = (pos[:, :, None] * inv[None, None, :]).astype(f32)
    cc, ss = np.cos(ang).astype(f32), np.sin(ang).astype(f32)
    ksc = f32(128.0 ** -0.5)
    m["rope"] = np.ascontiguousarray(np.stack([cc, ss, cc * ksc, ss * ksc], 2)).astype(f32)
    jj, ii = np.arange(128)[:, None], np.arange(128)[None, :]
    m["atc"] = np.ascontiguousarray(np.stack([np.maximum(ii - jj, 0), (ii >= jj), np.maximum(jj - ii, 0), (jj > ii)], 1)).astype(f32)
    i_ = np.arange(128, dtype=f32)
    m["xz"] = np.stack([i_ + 1, 128 - i_, 127 - i_, i_], 1).astype(f32)
    cs_, ce_ = NS_OWN * core + 1, NS_OWN * core + NS_OWN
    sic = np.zeros((128, 4, 1 + NSL), f32)
    sic[:16, 0, 0] = (C - 1 - (112 + np.arange(16))) + C * (cs_ - 1)
    sic[:16, 1, 0] = 1.0
    for s, g in enumerate(slots):
        if g < 0:
            continue
        q = g + 1
        if q < cs_:
            sic[:, 0, 1 + s] = (C - 1 - i_) + C * (cs_ - 1 - q)
            sic[:, 1, 1 + s] = 1.0
        if q > ce_:
            sic[:, 2, 1 + s] = i_ + C * (q - ce_ - 1)
            sic[:, 3, 1 + s] = 1.0
    m["sic"] = sic
    sip = np.zeros((128, 2), f32)
    sip[:16, 0] = C - 1 - (112 + np.arange(16))
    sip[:16, 1] = 1.0
    m["sip"] = sip
    rpb = inp["na_rpb"][0]
    kr_, kc_ = np.arange(128) // 64, np.arange(128) % 64
    bu = np.zeros((128, 7, 16, 128), f32)
    for dlt in range(-3, 4):
        dr = 2 * dlt + kr_[:, None] - kr_[None, :]
        dc = kc_[:, None] - kc_[None, :]
        ok = (np.abs(dr) <= 7) & (np.abs(dc) <= 15)
        g = rpb[:, np.clip(dr + 7, 0, 14), np.clip(dc + 15, 0, 30)]
        bu[:, dlt + 3] = np.where(ok[None], g, 0.0).transpose(1, 0, 2)
    m["biasu"] = bu
    m["maskp"] = cfg.maskp
    ms = np.full((128, NS_OWN * 7, 128), NEG, f32)
    for jl in range(NS_OWN):
        for dlt in range(-3, 4):
            mm = na_mask(2 * NT_S, NS_OWN * core + jl, dlt)
            if mm is not None:
                ms[:, jl * 7 + dlt + 3] = mm
    m["masks"] = ms
    return {k: np.ascontiguousarray(v, dtype=f32) for k, v in m.items()}


_CACHE = {}


def run(cfg, inp):
    cfg.pvar, cfg.maskp = prompt_masks(cfg.NT_P)
    cfg.NVP = cfg.maskp.shape[1]
    kb = build(cfg)
    in_maps = [host_inputs(cfg, c, inp) for c in range(8)]
    res = run_bass_kernel_spmd(kb.nc, in_maps, core_ids=list(range(8)))
    return res.results


def kernel(**inputs):
    inp = {k: np.asarray(v) for k, v in inputs.items()}
    cfg = Cfg()
    res = run(cfg, inp)
    NT_P, NS_OWN = cfg.NT_P, cfg.NS_OWN
    yp = np.stack([res[c]["y"][:NT_P * 128] for c in range(8)], 0).astype(np.float32)
    ys = np.concatenate([res[c]["y"][NT_P * 128:] for c in range(8)], 0)[None].astype(np.float32)
    return yp, ys
```

```python
import contextlib
import os
import numpy as np
import concourse.bass as bass
import concourse.mybir as mybir
from concourse.bass_utils import run_bass_kernel_spmd

F32 = mybir.dt.float32
BF16 = mybir.dt.bfloat16
ALU = mybir.AluOpType
AF = mybir.ActivationFunctionType

D = 2048
KC = 16
C = 128
N_META = 16
ALPHA = float(2.0 ** 0.25)
EPS = 1e-5
NEG = -30000.0
NRING = 24


class Cfg:
    def __init__(self, NT_P=32, NT_S=64, DFF=5632, debug=False):
        self.NT_P, self.NT_S, self.DFF, self.debug = NT_P, NT_S, DFF, debug
        self.NS_OWN = NT_S // 8
        self.NWIN = self.NS_OWN + 6
        self.NREST = NT_S - (self.NS_OWN + 3)
        self.NSL = self.NWIN + self.NREST
        self.NOWN = NT_P + self.NS_OWN
        self.NNA = NT_P + self.NWIN
        self.NT1 = 1 + NT_P + self.NSL
        self.FC = DFF // 128


class Buf:
    __slots__ = ("name", "w", "r")

    def __init__(self, name):
        self.name = name
        self.w = None
        self.r = {}


class Tile:
    def __init__(self, t, name):
        self.t = t
        self.b = Buf(name)


class Eng:
    def __init__(self, name, h):
        self.name, self.h = name, h
        self.sem = None
        self.cnt = 0
        self.pending = False
        self.waited = {}


class KB:
    def __init__(self):
        self.nc = bass.Bass("TRN2", target_bir_lowering=False)
        self.es = contextlib.ExitStack()
        nc = self.nc
        self.eng = {n: Eng(n, h) for n, h in [("pe", nc.tensor), ("act", nc.scalar), ("dve", nc.vector),
                                              ("pool", nc.gpsimd), ("sp", nc.sync)]}
        self.nsem = 0
        self.dq = {}
        for q in ("sp", "pool"):
            self.dq[q] = [[self.newsem() for _ in range(NRING)], 0]
        self.ndma = 0
        self.phase = None
        self.pstack = []
        self.recording = None

    def newsem(self):
        self.nsem += 1
        return (self.nsem, self.es.enter_context(self.nc.semaphore("s%d" % self.nsem)))

    def sb(self, name, shape, dt):
        st = self.phase if self.phase is not None else self.es
        self.nalloc = getattr(self, "nalloc", 0) + 1
        return Tile(st.enter_context(self.nc.sbuf_tensor("sb%d_%s" % (self.nalloc, name), list(shape), dt)), name)

    def push(self):
        self.pstack.append(contextlib.ExitStack())
        self.phase = self.pstack[-1]

    def pop(self):
        if os.environ.get("KB_VERBOSE"):
            print("phase end: sbuf bytes remaining", self.nc.sbuf_bytes_remaining)
        self.barrier()
        self.pstack.pop().close()
        self.phase = self.pstack[-1] if self.pstack else None

    def close_all(self):
        while self.pstack:
            self.pstack.pop().close()
        self.phase = None

    def barrier(self):
        evs = []
        for e in self.eng.values():
            assert not e.pending
            if e.sem is not None and e.cnt > 0:
                evs.append((e.sem[0], e.sem[1], e.cnt))
        for q in ("sp", "pool"):
            ring, i = self.dq[q]
            n = len(ring)
            for s_ in range(min(i, n)):
                evs.append((ring[s_][0], ring[s_][1], 16 * ((i - 1 - s_) // n + 1)))
        for e in self.eng.values():
            for ev in evs:
                self._need(e, ev)

    def ps(self, name, shape, dt):
        return Tile(self.es.enter_context(self.nc.psum_tensor("ps_" + name, list(shape), dt)), name)

    def _need(self, e, ev):
        k, h, v = ev
        if e.waited.get(k, 0) >= v:
            return
        e.h.wait_ge(h, v)
        e.waited[k] = v

    def _deps(self, e, r, w):
        for b in r:
            if b.w is not None:
                self._need(e, b.w[0])
        for b in w:
            if b.w is not None and not (e.name == "pe" and b.w[1] == "pe"):
                self._need(e, b.w[0])
            for en, ev in b.r.items():
                self._need(e, ev)

    def op(self, en, fn, r=(), w=(), inc=True):
        if self.recording is not None:
            self.recording.append(("op", (en, fn, r, w, inc)))
            return None
        e = self.eng[en]
        r = [x.b if isinstance(x, Tile) else x for x in r]
        w = [x.b if isinstance(x, Tile) else x for x in w]
        self._deps(e, r, w)
        if e.sem is None or (e.cnt >= 8000 and not e.pending):
            e.sem = self.newsem()
            e.cnt = 0
        ins = fn(e.h)
        if inc:
            ins.then_inc(e.sem[1], 1)
            e.cnt += 1
            e.pending = False
            ev = (e.sem[0], e.sem[1], e.cnt)
        else:
            e.pending = True
            ev = (e.sem[0], e.sem[1], e.cnt + 1)
        for b in r:
            b.r[en] = ev
        for b in w:
            b.w = (ev, en)
            b.r = {}
        return ins

    def dma(self, q, out, in_, r=(), w=()):
        if self.recording is not None:
            self.recording.append(("dma", (q, out, in_, r, w)))
            return
        e = self.eng[q]
        r = [x.b if isinstance(x, Tile) else x for x in r]
        w = [x.b if isinstance(x, Tile) else x for x in w]
        self._deps(e, r, w)
        ring, i = self.dq[q]
        n = len(ring)
        key, h = ring[i % n]
        val = 16 * (i // n + 1)
        if i >= n:
            self._need(e, (key, h, val - 16))
        e.h.dma_start(out=out, in_=in_).then_inc(h, 16)
        self.dq[q][1] = i + 1
        ev = (key, h, val)
        self.ndma += 1
        for b in r:
            b.r["dma%d" % self.ndma] = ev
        for b in w:
            b.w = (ev, "dma")
            b.r = {}

    def replay(self, rec):
        for kind, a in rec:
            (self.op if kind == "op" else self.dma)(*a)

    def finish(self):
        for q in ("sp", "pool"):
            ring, i = self.dq[q]
            n = len(ring)
            for s in range(min(i, n)):
                cnt = (i - 1 - s) // n + 1
                self._need(self.eng[q], (ring[s][0], ring[s][1], 16 * cnt))
        for e in self.eng.values():
            assert not e.pending, e.name


def build(cfg):
    kb = KB()
    nc = kb.nc
    NT_P, NT_S, DFF, FC = cfg.NT_P, cfg.NT_S, cfg.DFF, cfg.FC
    NS_OWN, NWIN, NREST, NSL, NOWN, NNA, NT1 = cfg.NS_OWN, cfg.NWIN, cfg.NREST, cfg.NSL, cfg.NOWN, cfg.NNA, cfg.NT1
    NVP = cfg.NVP

    def din(name, shape, dt=F32):
        return nc.dram_tensor(name, list(shape), dt, kind="ExternalInput").ap()

    def dscr(name, shape, dt, dbg=False):
        kind = "ExternalOutput" if (dbg and cfg.debug) else "Internal"
        return nc.dram_tensor(name, list(shape), dt, kind=kind).ap()

    xm = din("xm", [128, D])
    xp = din("xp", [NT_P * 128, D])
    xs = din("xs", [NSL * 128, D])
    w_in = din("w_in", [D, 7168])
    w_out = din("w_out", [D, D])
    w_g = din("w_g", [D, DFF])
    w_u = din("w_u", [D, DFF])
    w_d = din("w_d", [DFF, D])
    lnt = din("lnt", [6, 128, D])
    gng_d = din("gng", [128, 1024])
    dec_d = din("dec", [128, 16])
    ident_d = din("ident", [128, 128])
    rope_d = din("rope", [NT1, 128, 4, 64])
    atc_d = din("atc", [128, 4, 128])
    xz_d = din("xz", [128, 4])
    sic_d = din("sic", [128, 4, 1 + NSL])
    sip_d = din("sip", [128, 2])
    biasu_d = din("biasu", [128, 7, 16, 128])
    maskp_d = din("maskp", [128, NVP, 128])
    masks_d = din("masks", [128, NS_OWN * 7, 128])
    y_out = nc.dram_tensor("y", [NOWN * 128, D], F32, kind="ExternalOutput").ap()

    wb_in = dscr("wb_in", [D, 7168], BF16)
    wb_out = dscr("wb_out", [D, D], BF16)
    wb_g = dscr("wb_g", [D, DFF], BF16)
    wb_u = dscr("wb_u", [D, DFF], BF16)
    wb_d = dscr("wb_d", [DFF, D], BF16)
    qaT_s = dscr("qaT_s", [NOWN, 128, 8, 128], BF16, True)
    qrT_s = dscr("qrT_s", [NOWN, 3, 128, 8, 128], BF16, True)
    g_s = dscr("g_s", [NOWN, 128, 1024], F32, True)
    krT_s = dscr("krT_s", [NOWN, 128, 8, 128], BF16, True)
    kzf_s = dscr("kzf_s", [NOWN, 128, 1024], BF16, True)
    kzb_s = dscr("kzb_s", [NOWN, 128, 1024], BF16, True)
    vr_s = dscr("vr_s", [NOWN, 128, 1024], BF16, True)
    kaT_s = dscr("kaT_s", [NNA, 128, 8, 128], BF16, True)
    va_s = dscr("va_s", [NNA, 128, 1024], BF16, True)
    sb_s = dscr("sb_s", [NOWN, 128, 8, 128], BF16, True)
    mixT_s = dscr("mixT_s", [NOWN, 128, 16, 128], BF16, True)
    h_s = dscr("h_s", [NOWN, 128, D], F32, True)
    h1_s = dscr("h1_s", [NOWN, 128, D], F32, True)
    h1T_s = dscr("h1T_s", [NOWN, 128, 16, 128], BF16, True)
    sinit_s = dscr("sinit_s", [3, 128, 1024], F32, True)

    def bufs(prefix, n):
        return [Buf("%s%d" % (prefix, i)) for i in range(n)]

    B_qaT, B_g, B_krT, B_kzf, B_kzb, B_vr = (bufs(p, NOWN) for p in ("qaT", "g", "krT", "kzf", "kzb", "vr"))
    B_qrT = [bufs("qrT%d_" % i, 3) for i in range(NOWN)]
    B_kaT, B_va = bufs("kaT", NNA), bufs("va", NNA)
    B_sb, B_mixT, B_h, B_h1, B_h1T = (bufs(p, NOWN) for p in ("sb", "mixT", "h", "h1", "h1T"))

    cast_jobs = []

    def cast_weight(src, dst, rows, cols, name):
        chunks = []
        bw_ = 2048 if cols % 2048 == 0 else 512
        per = max(1, 8192 // (cols // bw_))
        r0 = 0
        while r0 < rows:
            r1 = min(rows, r0 + per)
            b = Buf("%s_%d" % (name, r0))
            cast_jobs.append(lambda r0=r0, r1=r1, b=b: kb.dma("pool", dst[r0:r1, :].rearrange("r (a b) -> (r a) b", b=bw_),
                                                            src[r0:r1, :].rearrange("r (a b) -> (r a) b", b=bw_), w=[b]))
            chunks.append(b)
            r0 = r1
        return chunks

    WB_out = cast_weight(w_out, wb_out, D, D, "wbout")
    WB_g = cast_weight(w_g, wb_g, D, DFF, "wbg")
    WB_u = cast_weight(w_u, wb_u, D, DFF, "wbu")
    WB_d = cast_weight(w_d, wb_d, DFF, D, "wbd")

    WB_in = {}

    def cast_in(lst):
        for u_, c0_ in lst:
            WB_in[u_] = Buf("wbin_" + u_)
            kb.dma("pool", wb_in[:, c0_:c0_ + 1024], w_in[:, c0_:c0_ + 1024], w=[WB_in[u_]])

    cast_in((("vr", 5120), ("kr", 4096), ("va", 2048), ("qr", 3072)))

    ident_f = kb.sb("ident_f", [128, 128], F32)
    ident = kb.sb("ident", [128, 128], BF16)
    kb.dma("sp", ident_f.t[:], ident_d[:, :], w=[ident_f])
    kb.op("dve", lambda e: e.tensor_copy(out=ident.t[:], in_=ident_f.t[:]), r=[ident_f], w=[ident])

    lng = kb.sb("lng", [128, D], F32)
    lnb = kb.sb("lnb", [128, D], F32)
    stat = kb.sb("stat", [128, 4, 6], F32)
    mv = kb.sb("mv", [128, 2], F32)
    rstd = kb.sb("rstd", [128, 1], F32)
    nmr = kb.sb("nmr", [128, 1], F32)
    lnx = kb.sb("lnx", [128, D], F32)
    kb.push()
    dec = kb.sb("dec", [128, 16], F32)
    kb.dma("sp", dec.t[:], dec_d[:, :], w=[dec])
    lg = kb.sb("lg", [128, 16], F32)
    tu = kb.sb("tu", [128, 16], F32)
    tw = kb.sb("tw", [128, 16], F32)
    td = kb.sb("td", [128, 16], F32)
    tl = kb.sb("tl", [128, 16], F32)
    kb.op("act", lambda e: e.activation(out=tu.t[:], in_=dec.t[:], func=AF.Exp, scale=-1.0), r=[dec], w=[tu])
    kb.op("dve", lambda e: e.tensor_scalar(out=tw.t[:], in0=tu.t[:], scalar1=1.0, scalar2=None, op0=ALU.add), r=[tu], w=[tw])
    kb.op("dve", lambda e: e.tensor_scalar(out=td.t[:], in0=tw.t[:], scalar1=-1.0, scalar2=1e-30, op0=ALU.add, op1=ALU.max), r=[tw], w=[td])
    kb.op("act", lambda e: e.activation(out=tl.t[:], in_=tw.t[:], func=AF.Ln), r=[tw], w=[tl])
    kb.op("dve", lambda e: e.reciprocal(out=td.t[:], in_=td.t[:]), r=[td], w=[td])
    kb.op("dve", lambda e: e.tensor_tensor(out=td.t[:], in0=tu.t[:], in1=td.t[:], op=ALU.mult), r=[tu, td], w=[td])
    kb.op("dve", lambda e: e.scalar_tensor_tensor(out=lg.t[:], in0=tl.t[:], scalar=-1.0, in1=td.t[:], op0=ALU.mult, op1=ALU.mult),
          r=[tl, td], w=[lg])

    kb.recording = []
    xz = kb.sb("xz", [128, 4], F32)
    kb.dma("sp", xz.t[:], xz_d[:, :], w=[xz])
    xif, xib, zef, zeb, gcf, gcb = (kb.sb(n, [128, 8], F32) for n in ("xif", "xib", "zef", "zeb", "gcf", "gcb"))
    for (dst, col, lo) in ((xif, 0, 0), (xib, 1, 8), (zef, 2, 0), (zeb, 3, 8)):
        kb.op("dve", lambda e, dst=dst, col=col, lo=lo: e.tensor_scalar(out=dst.t[:], in0=lg.t[:, lo:lo + 8], scalar1=xz.t[:, col:col + 1],
                                                                      scalar2=None, op0=ALU.mult), r=[lg, xz], w=[dst])
        kb.op("act", lambda e, dst=dst: e.activation(out=dst.t[:], in_=dst.t[:], func=AF.Exp), r=[dst], w=[dst])
    for (dst, lo) in ((gcf, 0), (gcb, 8)):
        kb.op("act", lambda e, dst=dst, lo=lo: e.activation(out=dst.t[:], in_=lg.t[:, lo:lo + 8], func=AF.Exp, scale=float(C)), r=[lg], w=[dst])

    atc = kb.sb("atc", [128, 4, 128], F32)
    kb.dma("sp", atc.t[:], atc_d[:, :, :], w=[atc])
    AT = kb.sb("AT", [128, 8, 128], F32)
    att = kb.sb("att", [128, 2, 128], F32)
    for h in range(8):
        kb.op("act", lambda e, h=h: e.activation(out=att.t[:, 0, :], in_=atc.t[:, 0, :], func=AF.Exp, scale=lg.t[:, h:h + 1]), r=[atc, lg], w=[att])
        kb.op("act", lambda e, h=h: e.activation(out=att.t[:, 1, :], in_=atc.t[:, 2, :], func=AF.Exp, scale=lg.t[:, 8 + h:9 + h]), r=[atc, lg], w=[att])
        kb.op("dve", lambda e: e.tensor_tensor(out=att.t[:, 0, :], in0=att.t[:, 0, :], in1=atc.t[:, 1, :], op=ALU.mult), r=[att, atc], w=[att])
        kb.op("dve", lambda e: e.tensor_tensor(out=att.t[:, 1, :], in0=att.t[:, 1, :], in1=atc.t[:, 3, :], op=ALU.mult), r=[att, atc], w=[att])
        kb.op("dve", lambda e, h=h: e.tensor_tensor(out=AT.t[:, h, :], in0=att.t[:, 0, :], in1=att.t[:, 1, :], op=ALU.add), r=[att], w=[AT])

    sic = kb.sb("sic", [128, 4, 1 + NSL], F32)
    kb.dma("sp", sic.t[:], sic_d[:, :, :], w=[sic])
    sip = kb.sb("sip", [128, 2], F32)
    kb.dma("sp", sip.t[:], sip_d[:, :], w=[sip])
    sWf = kb.sb("sWf", [128, 8, 1 + NSL], F32)
    sWb = kb.sb("sWb", [128, 8, 1 + NSL], F32)
    sPf = kb.sb("sPf", [128, 8], F32)
    for h in range(8):
        kb.op("act", lambda e, h=h: e.activation(out=sWf.t[:, h, :], in_=sic.t[:, 0, :], func=AF.Exp, scale=lg.t[:, h:h + 1]), r=[sic, lg], w=[sWf])
        kb.op("dve", lambda e, h=h: e.tensor_tensor(out=sWf.t[:, h, :], in0=sWf.t[:, h, :], in1=sic.t[:, 1, :], op=ALU.mult), r=[sWf, sic], w=[sWf])
        kb.op("act", lambda e, h=h: e.activation(out=sWb.t[:, h, :], in_=sic.t[:, 2, :], func=AF.Exp, scale=lg.t[:, 8 + h:9 + h]), r=[sic, lg], w=[sWb])
        kb.op("dve", lambda e, h=h: e.tensor_tensor(out=sWb.t[:, h, :], in0=sWb.t[:, h, :], in1=sic.t[:, 3, :], op=ALU.mult), r=[sWb, sic], w=[sWb])
    kb.op("dve", lambda e: e.tensor_scalar(out=sPf.t[:], in0=lg.t[:, 0:8], scalar1=sip.t[:, 0:1], scalar2=None, op0=ALU.mult), r=[lg, sip], w=[sPf])
    kb.op("act", lambda e: e.activation(out=sPf.t[:], in_=sPf.t[:], func=AF.Exp), r=[sPf], w=[sPf])
    kb.op("dve", lambda e: e.tensor_scalar(out=sPf.t[:], in0=sPf.t[:], scalar1=sip.t[:, 1:2], scalar2=None, op0=ALU.mult), r=[sPf, sip], w=[sPf])

    late_consts = kb.recording
    kb.recording = None

    def load_ln(i):
        kb.dma("sp", lng.t[:], lnt[2 * i, :, :], w=[lng])
        kb.dma("sp", lnb.t[:], lnt[2 * i + 1, :, :], w=[lnb])

    PB = [kb.ps("pb%d" % i, [128, 512], F32) for i in range(6)]
    PT = [kb.ps("pt%d" % i, [128, 1024], BF16) for i in range(2)]
    ptc = [0]

    def layer_norm(src, dst_f32, dst_bf=None):
        for c4 in range(4):
            kb.op("dve", lambda e, c4=c4: e.bn_stats(out=stat.t[:, c4, :], in_=src.t[:, c4 * 512:(c4 + 1) * 512]), r=[src], w=[stat])
        kb.op("dve", lambda e: e.bn_aggr(out=mv.t[:], in_=stat.t[:]), r=[stat], w=[mv])
        kb.op("act", lambda e: e.activation(out=rstd.t[:], in_=mv.t[:, 1:2], func=AF.Sqrt, bias=EPS, scale=1.0), r=[mv], w=[rstd])
        kb.op("dve", lambda e: e.reciprocal(out=rstd.t[:], in_=rstd.t[:]), r=[rstd], w=[rstd])
        kb.op("dve", lambda e: e.scalar_tensor_tensor(out=nmr.t[:], in0=mv.t[:, 0:1], scalar=-1.0, in1=rstd.t[:], op0=ALU.mult, op1=ALU.mult),
              r=[mv, rstd], w=[nmr])
        kb.op("act", lambda e: e.activation(out=lnx.t[:], in_=src.t[:], func=AF.Identity, bias=nmr.t[:], scale=rstd.t[:]), r=[src, nmr, rstd], w=[lnx])
        kb.op("pool", lambda e: e.tensor_tensor(out=lnx.t[:], in0=lnx.t[:], in1=lng.t[:], op=ALU.mult), r=[lnx, lng], w=[lnx])
        if dst_f32 is None:
            kb.op("pool", lambda e: e.tensor_tensor(out=dst_bf.t[:], in0=lnx.t[:], in1=lnb.t[:], op=ALU.add), r=[lnx, lnb], w=[dst_bf])
            return
        kb.op("dve", lambda e: e.tensor_tensor(out=dst_f32.t[:], in0=lnx.t[:], in1=lnb.t[:], op=ALU.add), r=[lnx, lnb], w=[dst_f32])
        if dst_bf is not None:
            kb.op("act", lambda e: e.activation(out=dst_bf.t[:], in_=dst_f32.t[:], func=AF.Copy), r=[dst_f32], w=[dst_bf])

    def transpose_cols(src, dstT_ap_fn, nblk, wbuf, evac="dve", src_off=0):
        for g0 in range(0, nblk, 8):
            n = min(8, nblk - g0)
            pt = PT[ptc[0] % 2]
            ptc[0] += 1
            for c in range(n):
                kb.op("pe", lambda e, c=c, g0=g0: e.transpose(out=pt.t[:, c * 128:(c + 1) * 128],
                                                                in_=src.t[:, src_off + (g0 + c) * 128: src_off + (g0 + c + 1) * 128],
                                                                identity=ident.t[:]),
                      r=[src, ident], w=[pt], inc=(c == n - 1))
            kb.op(evac, lambda e, g0=g0, n=n: e.tensor_copy(out=dstT_ap_fn(g0, n), in_=pt.t[:, 0:n * 128].rearrange("p (c t) -> p c t", t=128)),
                  r=[pt], w=[wbuf])


    if getattr(cfg, "stop", 99) == 0:
        kb.finish()
        kb.close_all()
        return kb
    load_ln(0)
    va_meta = kb.sb("va_meta", [16, 16, 66], BF16)
    kaT_meta = kb.sb("kaT_meta", [128, 8, 16], BF16)
    kb.op("dve", lambda e: e.memset(va_meta.t[:], 1.0), w=[va_meta])
    SA = [kb.sb("SA%d" % i, [128, 8, 128], F32) for i in range(3)]
    for t_ in SA:
        kb.op("pool", lambda e, t_=t_: e.memset(t_.t[:], 0.0), w=[t_])

    tiles = [dict(kind="meta", x=xm[:, :], rope=0)]
    for j in range(NT_P):
        tiles.append(dict(kind="own", x=xp[j * 128:(j + 1) * 128, :], rope=1 + j, own=j, na=j))
    for s_ in range(NSL):
        d = dict(x=xs[s_ * 128:(s_ + 1) * 128, :], rope=1 + NT_P + s_, slot=1 + s_)
        if s_ < NWIN:
            d["na"] = NT_P + s_
            if 3 <= s_ < 3 + NS_OWN:
                d["kind"] = "own"
                d["own"] = NT_P + (s_ - 3)
            else:
                d["kind"] = "halo"
        else:
            d["kind"] = "rest"
        tiles.append(d)
    UNITS = {"meta": ("va", "ka", "vr", "kr"), "own": ("va", "ka", "vr", "kr", "qa", "qr", "gr"),
             "halo": ("va", "ka", "vr", "kr"), "rest": ("vr", "kr")}
    UCOL = {"qa": 0, "ka": 1024, "va": 2048, "qr": 3072, "kr": 4096, "vr": 5120, "gr": 6144}
    NB1 = 4
    kb.push()
    xin = [kb.sb("xin%d" % i, [128, D], F32) for i in range(2)]
    NHB = 3
    hb = [kb.sb("hb%d" % i, [128, D], BF16) for i in range(NHB)]
    hTs = [kb.sb("hT%d" % i, [128, KC, 128], BF16) for i in range(2 * NB1)]
    wsl = [kb.sb("wsl%d" % i, [128, KC, 512], BF16) for i in range(3)]
    vtok = [kb.sb("vtok%d" % i, [128, 1024], BF16) for i in range(NB1)]
    stg = [kb.sb("stg%d" % i, [128, 1024], BF16) for i in range(3)]
    stgT = [kb.sb("stgT%d" % i, [128, 8, 128], BF16) for i in range(3)]
    ropet = [kb.sb("ropet%d" % i, [128, 4, 64], F32) for i in range(2 * NB1)]
    rof = [kb.sb("rof%d" % i, [128, 8, 128], F32) for i in range(2)]
    rt = [kb.sb("rt%d" % i, [128, 4, 64], F32) for i in range(4)]
    gst = [kb.sb("gst%d" % i, [128, 1024], F32) for i in range(1)]
    cnt = dict(stg=0, stgT=0, rof=0, gst=0, w=0, pb=0, x=0)

    def nxt(lst, key):
        v = lst[cnt[key] % len(lst)]
        cnt[key] += 1
        return v

    def store_T(src_bf, dst_ap, dbuf, src_off=0):
        tT = nxt(stgT, "stgT")
        transpose_cols(src_bf, lambda g0, n: tT.t[:, g0:g0 + n, :], 8, tT, src_off=src_off)
        kb.dma("pool", dst_ap, tT.t[:], r=[tT], w=[dbuf])
        return tT

    def rope_unit(pa, pb, tb, tblo, dst):
        for half, p in enumerate((pa, pb)):
            pv = p.t[:].rearrange("p (h d) -> p h d", d=128)
            x1, x2 = pv[:, :, 0:64], pv[:, :, 64:128]
            cc = tb.t[:, tblo:tblo + 1, :].broadcast_to([128, 4, 64])
            ss = tb.t[:, tblo + 1:tblo + 2, :].broadcast_to([128, 4, 64])
            o = dst.t[:, half * 4:(half + 1) * 4, :]
            kb.op("dve", lambda e: e.tensor_tensor(out=rt[0].t[:], in0=x1, in1=cc, op=ALU.mult), r=[p, tb], w=[rt[0]])
            kb.op("dve", lambda e: e.tensor_tensor(out=rt[1].t[:], in0=x2, in1=ss, op=ALU.mult), r=[p, tb], w=[rt[1]])
            kb.op("dve", lambda e: e.tensor_tensor(out=rt[2].t[:], in0=x1, in1=ss, op=ALU.mult), r=[p, tb], w=[rt[2]])
            kb.op("dve", lambda e: e.tensor_tensor(out=rt[3].t[:], in0=x2, in1=cc, op=ALU.mult), r=[p, tb], w=[rt[3]])
            kb.op("pool", lambda e: e.tensor_tensor(out=o[:, :, 0:64], in0=rt[0].t[:], in1=rt[1].t[:], op=ALU.subtract), r=[rt[0], rt[1]], w=[dst])
            kb.op("pool", lambda e: e.tensor_tensor(out=o[:, :, 64:128], in0=rt[2].t[:], in1=rt[3].t[:], op=ALU.add), r=[rt[2], rt[3]], w=[dst])

    def scaled_bf(srcf, tab, dst_bf):
        kb.op("dve", lambda e: e.tensor_tensor(out=dst_bf.t[:].rearrange("p (h d) -> p h d", d=128), in0=srcf.t[:],
                                               in1=tab.unsqueeze(2).broadcast_to([128, 8, 128]), op=ALU.mult), r=[srcf], w=[dst_bf])

    def state_contrib(kf, vt, tab_ap, tab_tile, acc):
        kw = nxt(stg, "stg")
        kb.op("dve", lambda e: e.tensor_tensor(out=kw.t[:].rearrange("p (h d) -> p h d", d=128), in0=kf.t[:],
                                               in1=tab_ap.unsqueeze(2).broadcast_to([128, 8, 128]), op=ALU.mult), r=[kf, tab_tile], w=[kw])
        for half in range(2):
            p = PB[4 + half]
            for hh in range(4):
                h = half * 4 + hh
                kb.op("pe", lambda e, h=h, hh=hh, p=p: e.matmul(p.t[:, hh * 128:(hh + 1) * 128], kw.t[:, h * 128:(h + 1) * 128],
                                                               vt.t[:, h * 128:(h + 1) * 128], start=True, stop=True),
                      r=[kw, vt], w=[p], inc=(hh == 3))
            kb.op("dve", lambda e, half=half, p=p: e.tensor_tensor(out=acc.t[:, half * 4:(half + 1) * 4, :], in0=acc.t[:, half * 4:(half + 1) * 4, :],
                                                                  in1=p.t[:].rearrange("p (h d) -> p h d", d=128), op=ALU.add), r=[acc, p], w=[acc])

    blocks = [tiles[b0:b0 + NB1] for b0 in range(0, NT1, NB1)]
    lnstate = {}

    def prep_ln(bi, ti):
        td_ = blocks[bi][ti]
        sl = (bi % 2) * NB1 + ti
        xt = nxt(xin, "x")
        hb_ = hb[(cnt["x"] - 1) % NHB]
        kb.dma("sp", xt.t[:], td_["x"], w=[xt])
        kb.dma("sp", ropet[sl].t[:], rope_d[td_["rope"], :, :, :], w=[ropet[sl]])
        if td_["kind"] == "own":
            layer_norm(xt, xt, hb_)
            kb.dma("pool", h_s[td_["own"], :, :], xt.t[:], r=[xt], w=[B_h[td_["own"]]])
        else:
            layer_norm(xt, None, hb_)
        lnstate[(bi, ti)] = hb_

    def prep_tr(bi, ti):
        sl = (bi % 2) * NB1 + ti
        hb_ = lnstate.pop((bi, ti))
        transpose_cols(hb_, lambda g0, n, sl=sl: hTs[sl].t[:, g0:g0 + n, :], KC, hTs[sl])

    def prep_stages(bi):
        if bi >= len(blocks):
            return []
        n = len(blocks[bi])
        acts = []
        for i in range(n):
            acts.append(("ln", i))
            if i >= NHB - 1:
                acts.append(("tr", i - (NHB - 1)))
        for i in range(max(0, n - (NHB - 1)), n):
            acts.append(("tr", i))
        k0 = min(n, NHB)
        st0 = [a for a in acts[:k0 + (1 if n >= NHB else 0)] if a[0] == "ln"][:k0]
        rest = [a for a in acts if a not in st0]
        st = [st0]
        if rest:
            st.append(rest[:-1])
            st.append(rest[-1:])
        return st

    def run_stage(bi, acts):
        for a, ti in acts:
            (prep_ln if a == "ln" else prep_tr)(bi, ti)

    sWc = kb.sb("sWc", [128, 8, 1 + NSL], F32)

    def state_contrib2(kf, vt, sl):
        kw = nxt(stg, "stg")
        kb.op("dve", lambda e: e.tensor_tensor(out=kw.t[:].rearrange("p (h d) -> p h d", d=128), in0=kf.t[:],
                                               in1=sWc.t[:, :, sl].unsqueeze(2).broadcast_to([128, 8, 128]), op=ALU.mult), r=[kf, sWc], w=[kw])
        for half in range(2):
            p = PB[4 + half]
            for hh in range(4):
                h = half * 4 + hh
                kb.op("pe", lambda e, h=h, hh=hh, p=p: e.matmul(p.t[:, hh * 128:(hh + 1) * 128], kw.t[:, h * 128:(h + 1) * 128],
                                                               vt.t[:, h * 128:(h + 1) * 128], start=True, stop=True),
                      r=[kw, vt], w=[p], inc=(hh == 3))
            for (acc, mcol) in ((SA[1], 1), (SA[2], 3)):
                kb.op("dve", lambda e, half=half, p=p, acc=acc, mcol=mcol: e.scalar_tensor_tensor(
                    out=acc.t[:, half * 4:(half + 1) * 4, :], in0=p.t[:].rearrange("p (h d) -> p h d", d=128), scalar=sic.t[:, mcol, sl:sl + 1],
                    in1=acc.t[:, half * 4:(half + 1) * 4, :], op0=ALU.mult, op1=ALU.add), r=[acc, p, sic], w=[acc])

    deferq = []

    def defer(fn):
        deferq.append(fn)

    def flush_defer():
        while deferq:
            deferq.pop(0)()

    pro = prep_stages(0)
    run_stage(0, pro.pop(0))
    kb.replay(late_consts)
    kb.op("pool", lambda e: e.tensor_tensor(out=sWc.t[:], in0=sWf.t[:], in1=sWb.t[:], op=ALU.add), r=[sWf, sWb], w=[sWc])
    for acts in pro:
        run_stage(0, acts)
    cast_in((("ka", 1024), ("qa", 0), ("gr", 6144)))
    for bi, blk in enumerate(blocks):
        units = [u for u in ("vr", "kr", "va", "qr", "ka", "qa", "gr") if any(u in UNITS[t["kind"]] for t in blk)]
        pending = prep_stages(bi + 1)
        per_unit = 1
        if pending:
            run_stage(bi + 1, pending.pop(0))
        for ui, u in enumerate(units):
            if ui > 0:
                for _ in range(per_unit):
                    if pending:
                        run_stage(bi + 1, pending.pop(0))
            ws = [nxt(wsl, "w"), nxt(wsl, "w")]
            for i2 in range(2):
                c0 = UCOL[u] + i2 * 512
                kb.dma("sp", ws[i2].t[:], wb_in[:, c0:c0 + 512].rearrange("(k p) c -> p k c", p=128), r=[WB_in[u]], w=[ws[i2]])
            for ti, td_ in enumerate(blk):
                if u not in UNITS[td_["kind"]]:
                    continue
                kind = td_["kind"]
                pp = [PB[(cnt["pb"] % 2) * 2], PB[(cnt["pb"] % 2) * 2 + 1]]
                cnt["pb"] += 1
                for i2 in range(2):
                    for k in range(KC):
                        kb.op("pe", lambda e, i2=i2, k=k, ti=ti: e.matmul(pp[i2].t[:], hTs[(bi % 2) * NB1 + ti].t[:, k, :], ws[i2].t[:, k, :],
                                                                          start=(k == 0), stop=(k == KC - 1)),
                              r=[hTs[(bi % 2) * NB1 + ti], ws[i2]], w=[pp[i2]], inc=(k == KC - 1))
                flush_defer()
                if u in ("va", "ka", "vr", "qa"):
                    dst = vtok[ti] if u == "vr" else nxt(stg, "stg")
                    sc = 0.125 if u == "qa" else 1.0
                    for i2 in range(2):
                        kb.op("act", lambda e, i2=i2, dst=dst, sc=sc: e.activation(out=dst.t[:, i2 * 512:(i2 + 1) * 512], in_=pp[i2].t[:], func=AF.Copy, scale=sc),
                              r=[pp[i2]], w=[dst])
                    if u == "va":
                        if kind == "meta":
                            kb.op("dve", lambda e, dst=dst: e.tensor_copy(out=va_meta.t[:, :, 0:64], in_=dst.t[0:16, :].rearrange("p (h d) -> p h d", d=64)),
                                  r=[dst], w=[va_meta])
                        else:
                            kb.dma("pool", va_s[td_["na"], :, :], dst.t[:], r=[dst], w=[B_va[td_["na"]]])
                    elif u == "ka":
                        if kind == "meta":
                            def _km(dst=dst):
                                tT = nxt(stgT, "stgT")
                                transpose_cols(dst, lambda g0, n, tT=tT: tT.t[:, g0:g0 + n, :], 8, tT)
                                kb.op("dve", lambda e, tT=tT: e.tensor_copy(out=kaT_meta.t[:], in_=tT.t[:, :, 0:16]), r=[tT], w=[kaT_meta])
                            defer(_km)
                        else:
                            defer(lambda dst=dst, td_=td_: store_T(dst, kaT_s[td_["na"], :, :, :], B_kaT[td_["na"]]))
                    elif u == "qa":
                        defer(lambda dst=dst, td_=td_: store_T(dst, qaT_s[td_["own"], :, :, :], B_qaT[td_["own"]]))
                    elif u == "vr" and kind == "own":
                        kb.dma("pool", vr_s[td_["own"], :, :], dst.t[:], r=[dst], w=[B_vr[td_["own"]]])
                elif u == "gr":
                    gs = nxt(gst, "gst")
                    for i2 in range(2):
                        kb.op("act", lambda e, i2=i2, gs=gs: e.activation(out=gs.t[:, i2 * 512:(i2 + 1) * 512], in_=pp[i2].t[:], func=AF.Silu), r=[pp[i2]], w=[gs])
                    kb.dma("pool", g_s[td_["own"], :, :], gs.t[:], r=[gs], w=[B_g[td_["own"]]])
                elif u == "kr":
                    kf = nxt(rof, "rof")
                    rope_unit(pp[0], pp[1], ropet[(bi % 2) * NB1 + ti], 2, kf)
                    if kind == "own":
                        o = td_["own"]
                        kbf = nxt(stg, "stg")
                        kb.op("act", lambda e, kbf=kbf, kf=kf: e.activation(out=kbf.t[:], in_=kf.t[:].rearrange("p h d -> p (h d)"), func=AF.Copy), r=[kf], w=[kbf])
                        defer(lambda kbf=kbf, o=o: store_T(kbf, krT_s[o, :, :, :], B_krT[o]))
                        for (tab, dsc, bb) in ((zef, kzf_s, B_kzf), (zeb, kzb_s, B_kzb)):
                            kz = nxt(stg, "stg")
                            kb.op("dve", lambda e, kz=kz, tab=tab, kf=kf: e.tensor_tensor(out=kz.t[:].rearrange("p (h d) -> p h d", d=128), in0=kf.t[:],
                                                                                       in1=tab.t[:].unsqueeze(2).broadcast_to([128, 8, 128]), op=ALU.mult),
                                  r=[kf, tab], w=[kz])
                            kb.dma("pool", dsc[o, :, :], kz.t[:], r=[kz], w=[bb[o]])
                    elif kind == "meta":
                        defer(lambda kf=kf, ti=ti: state_contrib(kf, vtok[ti], sPf.t[:, :], sPf, SA[0]))
                        defer(lambda kf=kf, ti=ti: state_contrib(kf, vtok[ti], sWf.t[:, :, 0], sWf, SA[1]))
                    else:
                        sl = td_["slot"]
                        defer(lambda kf=kf, ti=ti, sl=sl: state_contrib2(kf, vtok[ti], sl))
                elif u == "qr":
                    o = td_["own"]
                    qf = nxt(rof, "rof")
                    rope_unit(pp[0], pp[1], ropet[(bi % 2) * NB1 + ti], 0, qf)
                    q0 = nxt(stg, "stg")
                    kb.op("act", lambda e, q0=q0, qf=qf: e.activation(out=q0.t[:], in_=qf.t[:].rearrange("p h d -> p (h d)"), func=AF.Copy), r=[qf], w=[q0])
                    defer(lambda q0=q0, o=o: store_T(q0, qrT_s[o, 0, :, :, :], B_qrT[o][0]))
                    for vi, tab in ((1, xif), (2, xib)):
                        qv = nxt(stg, "stg")
                        kb.op("dve", lambda e, qv=qv, tab=tab, qf=qf: e.tensor_tensor(out=qv.t[:].rearrange("p (h d) -> p h d", d=128), in0=qf.t[:],
                                                                                   in1=tab.t[:].unsqueeze(2).broadcast_to([128, 8, 128]), op=ALU.mult),
                              r=[qf, tab], w=[qv])
                        def _qv(qv=qv, o=o, vi=vi):
                            tT = nxt(stgT, "stgT")
                            transpose_cols(qv, lambda g0, n, tT=tT: tT.t[:, g0:g0 + n, :], 8, tT)
                            kb.dma("pool", qrT_s[o, vi, :, :, :], tT.t[:], r=[tT], w=[B_qrT[o][vi]])
                        defer(_qv)
        flush_defer()
        while pending:
            run_stage(bi + 1, pending.pop(0))
        if bi >= 2 and bi % 2 == 0 and cast_jobs:
            cast_jobs.pop(0)()
    for i in range(3):
        kb.dma("pool", sinit_s[i, :, :], SA[i].t[:].rearrange("p h d -> p (h d)"), r=[SA[i]])


    if getattr(cfg, "stop", 99) == 1:
        kb.finish()
        kb.close_all()
        return kb
    kb.pop()
    kb.push()
    while cast_jobs:
        cast_jobs.pop(0)()

    Sst = kb.sb("Sst", [128, 8, 128], F32)
    Sbf = [kb.sb("Sbf%d" % i, [128, 8, 128], BF16) for i in range(2)]
    ldk = [kb.sb("ldk%d" % i, [128, 1024], BF16) for i in range(2)]
    ldv = [kb.sb("ldv%d" % i, [128, 1024], BF16) for i in range(2)]
    cnt.update(sbf=0, ld=0)

    def state_step(S, kz_t, v_t, gc):
        for half in range(2):
            p = PB[4 + half]
            for hh in range(4):
                h = half * 4 + hh
                kb.op("pe", lambda e, h=h, hh=hh, p=p: e.matmul(p.t[:, hh * 128:(hh + 1) * 128], kz_t.t[:, h * 128:(h + 1) * 128],
                                                               v_t.t[:, h * 128:(h + 1) * 128], start=True, stop=True),
                      r=[kz_t, v_t], w=[p], inc=(hh == 3))
            for hh in range(4):
                h = half * 4 + hh
                kb.op("dve", lambda e, h=h, hh=hh, p=p: e.scalar_tensor_tensor(out=S.t[:, h, :], in0=S.t[:, h, :], scalar=gc.t[:, h:h + 1],
                                                                              in1=p.t[:, hh * 128:(hh + 1) * 128], op0=ALU.mult, op1=ALU.add),
                      r=[S, gc, p], w=[S])

    seqs = [(0, NT_P, None), (NT_P, NOWN, SA[2])]
    Gb = kb.sb("Gb", [128, 8, 128], F32)
    kb.op("dve", lambda e: e.tensor_copy(out=Gb.t[:], in_=gcb.t[:].unsqueeze(2).broadcast_to([128, 8, 128])), r=[gcb], w=[Gb])
    Spp = [Sst, kb.sb("Sst2", [128, 8, 128], F32)]
    ldk = ldk + [kb.sb("ldk%d" % i, [128, 1024], BF16) for i in range(2, 4)]
    ldv = ldv + [kb.sb("ldv%d" % i, [128, 1024], BF16) for i in range(2, 4)]
    Sbf = Sbf + [kb.sb("Sbf%d" % i, [128, 8, 128], BF16) for i in range(2, 4)]
    cnt.update(pp2=0, spp=0)

    def state_step2(S_in, S_out, kz_t, v_t):
        ps_ = []
        for half in range(2):
            p = PB[cnt["pp2"] % 6]
            cnt["pp2"] += 1
            for hh in range(4):
                h = half * 4 + hh
                kb.op("pe", lambda e, h=h, hh=hh, p=p: e.matmul(p.t[:, hh * 128:(hh + 1) * 128], kz_t.t[:, h * 128:(h + 1) * 128],
                                                               v_t.t[:, h * 128:(h + 1) * 128], start=True, stop=True),
                      r=[kz_t, v_t], w=[p], inc=(hh == 3))
            ps_.append(p)
        kb.op("dve", lambda e: e.tensor_tensor(out=S_out.t[:], in0=S_in.t[:], in1=Gb.t[:], op=ALU.mult), r=[S_in, Gb], w=[S_out])
        for half in range(2):
            p = ps_[half]
            kb.op("dve", lambda e, half=half, p=p: e.tensor_tensor(out=S_out.t[:, half * 4:(half + 1) * 4, :], in0=S_out.t[:, half * 4:(half + 1) * 4, :],
                                                                  in1=p.t[:].rearrange("p (h d) -> p h d", d=128), op=ALU.add), r=[S_out, p], w=[S_out])

    for (o0, o1, init) in seqs:
        S_cur = Spp[cnt["spp"] % 2]
        if init is None:
            kb.op("pool", lambda e, S_cur=S_cur: e.memset(S_cur.t[:], 0.0), w=[S_cur])
        else:
            kb.op("pool", lambda e, init=init, S_cur=S_cur: e.tensor_copy(out=S_cur.t[:], in_=init.t[:]), r=[init], w=[S_cur])
        for o in range(o1 - 1, o0 - 1, -1):
            sbf = nxt(Sbf, "sbf")
            kb.op("act", lambda e, sbf=sbf, S_cur=S_cur: e.activation(out=sbf.t[:], in_=S_cur.t[:], func=AF.Copy), r=[S_cur], w=[sbf])
            kb.dma("pool", sb_s[o, :, :, :], sbf.t[:], r=[sbf], w=[B_sb[o]])
            if o > o0:
                kz_t, v_t = ldk[cnt["ld"] % 4], ldv[cnt["ld"] % 4]
                cnt["ld"] += 1
                kb.dma("sp", kz_t.t[:], kzb_s[o, :, :], r=[B_kzb[o]], w=[kz_t])
                kb.dma("sp", v_t.t[:], vr_s[o, :, :], r=[B_vr[o]], w=[v_t])
                cnt["spp"] += 1
                S_nxt = Spp[cnt["spp"] % 2]
                state_step2(S_cur, S_nxt, kz_t, v_t)
                S_cur = S_nxt

    if getattr(cfg, "stop", 99) == 2:
        kb.finish()
        kb.close_all()
        return kb
    kb.pop()
    kb.push()
    stgT = [kb.sb("stgT%d" % i, [128, 8, 128], BF16) for i in range(3)]
    gst = [kb.sb("gst%d" % i, [128, 1024], F32) for i in range(1)]
    biasu = kb.sb("biasu", [128, 7, 16, 128], BF16)
    maskp = kb.sb("maskp", [128, NVP, 128], BF16)
    masks = kb.sb("masks", [128, NS_OWN * 7, 128], BF16)
    for dlt in range(7):
        for hq in range(4):
            tmpf = nxt(gst, "gst")
            kb.dma("sp", tmpf.t[:, 0:512], biasu_d[:, dlt, hq * 4:(hq + 1) * 4, :].rearrange("p h q -> p (h q)"), w=[tmpf])
            kb.op("dve", lambda e, dlt=dlt, hq=hq, tmpf=tmpf: e.tensor_copy(out=biasu.t[:, dlt, hq * 4:(hq + 1) * 4, :].rearrange("p h q -> p (h q)"),
                                                                          in_=tmpf.t[:, 0:512]), r=[tmpf], w=[biasu])
    for (src_d, dst_t, nv) in ((maskp_d, maskp, NVP), (masks_d, masks, NS_OWN * 7)):
        for v0 in range(0, nv, 8):
            n = min(8, nv - v0)
            tmpf = nxt(gst, "gst")
            kb.dma("sp", tmpf.t[:, 0:n * 128], src_d[:, v0:v0 + n, :].rearrange("p v q -> p (v q)"), w=[tmpf])
            kb.op("dve", lambda e, v0=v0, n=n, tmpf=tmpf, dst_t=dst_t: e.tensor_copy(out=dst_t.t[:, v0:v0 + n, :].rearrange("p v q -> p (v q)"),
                                                                                  in_=tmpf.t[:, 0:n * 128]), r=[tmpf], w=[dst_t])
    gng = kb.sb("gng", [128, 1024], F32)
    kb.dma("sp", gng.t[:], gng_d[:, :], w=[gng])

    NRK = 8
    rk = [kb.sb("rk%d" % i, [128, 8, 128], BF16) for i in range(NRK)]
    rv = [kb.sb("rv%d" % i, [128, 16, 66], BF16) for i in range(NRK)]
    for t_ in rv:
        kb.op("pool", lambda e, t_=t_: e.memset(t_.t[:], 1.0), w=[t_])
    ring_of = {}
    cnt.update(ring=0, q=0, pt=0, nb=0)

    def get_key(na):
        if na in ring_of and ring_of[na][2] == na:
            return ring_of[na][0], ring_of[na][1]
        i = cnt["ring"] % NRK
        cnt["ring"] += 1
        for k_ in list(ring_of):
            if ring_of[k_][3] == i:
                del ring_of[k_]
        kb.dma("sp", rk[i].t[:], kaT_s[na, :, :, :], r=[B_kaT[na]], w=[rk[i]])
        kb.dma("sp", rv[i].t[:, :, 0:64], va_s[na, :, :].rearrange("p (h d) -> p h d", d=64), r=[B_va[na]], w=[rv[i]])
        ring_of[na] = (rk[i], rv[i], na, i)
        return rk[i], rv[i]

    qa_t = [kb.sb("qa_t%d" % i, [128, 8, 128], BF16) for i in range(2)]
    qo_t = [kb.sb("qo_t%d" % i, [128, 8, 128], BF16) for i in range(2)]
    for t_ in qa_t + qo_t:
        kb.op("pool", lambda e, t_=t_: e.memset(t_.t[:], 0.0), w=[t_])
    qr_t = [kb.sb("qr_t%d" % i, [128, 3, 8, 128], BF16) for i in range(1)]
    g_t = [kb.sb("g_t%d" % i, [128, 1024], F32) for i in range(1)]
    kr_t = [kb.sb("kr_t%d" % i, [128, 8, 128], BF16) for i in range(1)]
    kzf_t = [kb.sb("kzf_t%d" % i, [128, 1024], BF16) for i in range(1)]
    vr_t = [kb.sb("vr_t%d" % i, [128, 1024], BF16) for i in range(1)]
    sb_t = [kb.sb("sb_t%d" % i, [128, 8, 128], BF16) for i in range(1)]
    PTs = [kb.sb("PTs%d" % i, [128, 4, 128], BF16) for i in range(17)]
    PTm = [kb.sb("PTm%d" % i, [16, 4, 128], BF16) for i in range(2)]
    mix = [kb.sb("mix%d" % i, [128, D], BF16) for i in range(2)]
    rec = kb.sb("rec", [128, 4], F32)
    orf = kb.sb("orf", [128, 8, 128], F32)
    gst6 = kb.sb("gst6", [128, 8, 6], F32)
    gmv = kb.sb("gmv", [128, 8, 2], F32)
    grs = kb.sb("grs", [128, 8], F32)
    Sf = kb.sb("Sf", [128, 8, 128], F32)
    Sfb = kb.sb("Sfb", [128, 8, 128], BF16)

    class _Stop(Exception):
        pass

    def chk(n):
        if getattr(cfg, "stop", 99) == n:
            raise _Stop()

    def p3_body():
      tail_q = []
      for si, (o0, o1, _) in enumerate(seqs):
          kb.op("pool", lambda e, si=si: e.tensor_copy(out=Sf.t[:], in_=SA[si].t[:]), r=[SA[si]], w=[Sf])
          kb.op("act", lambda e: e.activation(out=Sfb.t[:], in_=Sf.t[:], func=AF.Copy), r=[Sf], w=[Sfb])
          ring_of.clear()
          for o in range(o0, o1):
              j = o - o0
              qi = 0
              cnt["q"] += 1
              q2 = cnt["q"] % 2
              qa, qr, gg, kr, kzf, vr, sbt, mx = qa_t[q2], qr_t[qi], g_t[qi], kr_t[qi], kzf_t[qi], vr_t[qi], sb_t[qi], mix[q2]
              qeo = (qa, qo_t[q2])
              kb.dma("sp", qa.t[0:64, :, :], qaT_s[o, 0:64, :, :], r=[B_qaT[o]], w=[qa])
              kb.dma("sp", qeo[1].t[64:128, :, :], qaT_s[o, 64:128, :, :], r=[B_qaT[o]], w=[qeo[1]])
              if si == 0:
                  dl = [(d_, cfg.pvar[(j, d_)]) for d_ in range(-3, 4) if (j, d_) in cfg.pvar]
                  keys = [(d_, get_key(j + d_), maskp.t[:, v, :], maskp) for d_, v in dl]
              else:
                  keys = [(d_, get_key(NT_P + 3 + j + d_), masks.t[:, j * 7 + d_ + 3, :], masks) for d_ in range(-3, 4)]
              kb.dma("sp", qr.t[:], qrT_s[o, :, :, :, :].rearrange("v p c t -> p v c t"), r=B_qrT[o], w=[qr])
              kb.dma("sp", kr.t[:], krT_s[o, :, :, :], r=[B_krT[o]], w=[kr])
              kb.dma("sp", kzf.t[:], kzf_s[o, :, :], r=[B_kzf[o]], w=[kzf])
              kb.dma("sp", vr.t[:], vr_s[o, :, :], r=[B_vr[o]], w=[vr])
              kb.dma("sp", sbt.t[:], sb_s[o, :, :, :], r=[B_sb[o]], w=[sbt])
              kb.dma("sp", gg.t[:], g_s[o, :, :], r=[B_g[o]], w=[gg])

              def na_scores(hg):
                  pts = []
                  for (d_, (rk_, rv_), mk, mkt) in keys:
                      p = PB[cnt["nb"] % 4]
                      cnt["nb"] += 1
                      for hh in range(4):
                          h = hg * 4 + hh
                          c, qq = h // 2, qeo[h % 2]
                          kb.op("pe", lambda e, hh=hh, c=c, qq=qq, rk_=rk_, p=p: e.matmul(p.t[:, hh * 128:(hh + 1) * 128], rk_.t[:, c, :],
                                                                                         qq.t[:, c, :], start=(hh == 0), stop=False,
                                                                                         skip_group_check=True),
                                r=[rk_, qq], w=[p], inc=False)
                      kb.op("pe", lambda e, d_=d_, p=p: e.matmul(p.t[:], ident.t[:], biasu.t[:, d_ + 3, hg * 4:(hg + 1) * 4, :].rearrange("p h q -> p (h q)"),
                                                                start=False, stop=False, skip_group_check=True), r=[ident, biasu], w=[p], inc=False)
                      for hh in range(4):
                          kb.op("pe", lambda e, mk=mk, p=p, hh=hh: e.matmul(p.t[:, hh * 128:(hh + 1) * 128], ident.t[:], mk, start=False, stop=True,
                                                                           skip_group_check=True),
                                r=[ident, mkt], w=[p], inc=(hh == 3))
                      pt_ = nxt(PTs, "pt")
                      kb.op("act", lambda e, pt_=pt_, p=p: e.activation(out=pt_.t[:].rearrange("p h q -> p (h q)"), in_=p.t[:], func=AF.Exp), r=[p], w=[pt_])
                      pts.append((pt_, rv_))
                  p = PB[cnt["nb"] % 4]
                  cnt["nb"] += 1
                  for hh in range(4):
                      h = hg * 4 + hh
                      c, qq = h // 2, qeo[h % 2]
                      kb.op("pe", lambda e, hh=hh, c=c, qq=qq, p=p: e.matmul(p.t[0:16, hh * 128:(hh + 1) * 128], kaT_meta.t[:, c, :],
                                                                            qq.t[:, c, :], start=True, stop=True),
                            r=[kaT_meta, qq], w=[p], inc=(hh == 3))
                  ptm = PTm[hg % 2]
                  kb.op("act", lambda e, ptm=ptm, p=p: e.activation(out=ptm.t[:].rearrange("p h q -> p (h q)"), in_=p.t[0:16, :], func=AF.Exp), r=[p], w=[ptm])
                  return pts, ptm

              def na_pv(hg, pts, ptm):
                  po_ = PB[4 + (hg % 2)]
                  for hh in range(4):
                      h = hg * 4 + hh
                      for ki, (pt_, rv_) in enumerate(pts):
                          kb.op("pe", lambda e, hh=hh, h=h, pt_=pt_, rv_=rv_, ki=ki: e.matmul(po_.t[:, hh * 65:(hh + 1) * 65], pt_.t[:, hh, :], rv_.t[:, h, 0:65],
                                                                                           start=(ki == 0), stop=False),
                                r=[pt_, rv_], w=[po_], inc=False)
                      kb.op("pe", lambda e, hh=hh, h=h: e.matmul(po_.t[:, hh * 65:(hh + 1) * 65], ptm.t[:, hh, :], va_meta.t[:, h, 0:65], start=False, stop=True),
                            r=[ptm, va_meta], w=[po_], inc=(hh == 3))
                  pov = po_.t[:, 0:260].rearrange("p (h d) -> p h d", d=65)
                  kb.op("dve", lambda e: e.reciprocal(out=rec.t[:].unsqueeze(2), in_=pov[:, :, 64:65]), r=[po_], w=[rec])
                  kb.op("dve", lambda e, hg=hg: e.tensor_tensor(out=mx.t[:, hg * 256:(hg + 1) * 256].rearrange("p (h d) -> p h d", d=64), in0=pov[:, :, 0:64],
                                                                in1=rec.t[:].unsqueeze(2).broadcast_to([128, 4, 64]), op=ALU.mult), r=[po_, rec], w=[mx])

              ret_pts = []

              def ret_scores():
                  for half in range(2):
                      p = PB[cnt["nb"] % 4]
                      cnt["nb"] += 1
                      for hh in range(4):
                          h = half * 4 + hh
                          kb.op("pe", lambda e, hh=hh, h=h, p=p: e.matmul(p.t[:, hh * 128:(hh + 1) * 128], kr.t[:, h, :], qr.t[:, 0, h, :], start=True, stop=True),
                                r=[kr, qr], w=[p], inc=(hh == 3))
                      pt_ = nxt(PTs, "pt")
                      kb.op("dve", lambda e, pt_=pt_, p=p, half=half: e.tensor_tensor(out=pt_.t[:], in0=p.t[:].rearrange("p (h q) -> p h q", q=128),
                                                                                    in1=AT.t[:, half * 4:(half + 1) * 4, :], op=ALU.mult), r=[p, AT], w=[pt_])
                      ret_pts.append(pt_)

              sc = na_scores(0)
              while tail_q:
                  tail_q.pop(0)()
              for hg in range(4):
                  nsc = na_scores(hg + 1) if hg < 3 else None
                  na_pv(hg, *sc)
                  sc = nsc
                  if hg == 1:
                      ret_scores()
              for half in range(2):
                  pt_ = ret_pts[half]
                  po_ = PB[4 + half]
                  for hh in range(4):
                      h = half * 4 + hh
                      oo = po_.t[:, hh * 128:(hh + 1) * 128]
                      kb.op("pe", lambda e, oo=oo, hh=hh, h=h, pt_=pt_: e.matmul(oo, pt_.t[:, hh, :], vr.t[:, h * 128:(h + 1) * 128], start=True, stop=False),
                            r=[pt_, vr], w=[po_], inc=False)
                      kb.op("pe", lambda e, oo=oo, h=h: e.matmul(oo, qr.t[:, 1, h, :], Sfb.t[:, h, :], start=False, stop=False), r=[qr, Sfb], w=[po_], inc=False)
                      kb.op("pe", lambda e, oo=oo, h=h: e.matmul(oo, qr.t[:, 2, h, :], sbt.t[:, h, :], start=False, stop=True), r=[qr, sbt], w=[po_], inc=(hh == 3))
                  kb.op("act", lambda e, half=half, po_=po_: e.activation(out=orf.t[:, half * 4:(half + 1) * 4, :].rearrange("p h d -> p (h d)"), in_=po_.t[:], func=AF.Copy),
                        r=[po_], w=[orf])
              state_step(Sf, kzf, vr, gcf)
              kb.op("act", lambda e: e.activation(out=Sfb.t[:], in_=Sf.t[:], func=AF.Copy), r=[Sf], w=[Sfb])
              for h in range(8):
                  kb.op("dve", lambda e, h=h: e.bn_stats(out=gst6.t[:, h, :], in_=orf.t[:, h, :]), r=[orf], w=[gst6])
              for h in range(8):
                  kb.op("dve", lambda e, h=h: e.bn_aggr(out=gmv.t[:, h, :], in_=gst6.t[:, h, :]), r=[gst6], w=[gmv])
              kb.op("act", lambda e: e.activation(out=grs.t[:].unsqueeze(2), in_=gmv.t[:, :, 1:2], func=AF.Sqrt, bias=EPS, scale=1.0), r=[gmv], w=[grs])
              kb.op("dve", lambda e: e.reciprocal(out=grs.t[:], in_=grs.t[:]), r=[grs], w=[grs])
              kb.op("dve", lambda e: e.tensor_tensor(out=orf.t[:], in0=orf.t[:], in1=gmv.t[:, :, 0:1].broadcast_to([128, 8, 128]), op=ALU.subtract), r=[orf, gmv], w=[orf])
              kb.op("dve", lambda e: e.tensor_tensor(out=orf.t[:], in0=orf.t[:], in1=grs.t[:].unsqueeze(2).broadcast_to([128, 8, 128]), op=ALU.mult), r=[orf, grs], w=[orf])
              orf2 = orf.t[:].rearrange("p h d -> p (h d)")
              kb.op("pool", lambda e, orf2=orf2: e.tensor_tensor(out=orf2, in0=orf2, in1=gng.t[:], op=ALU.mult), r=[orf, gng], w=[orf])
              kb.op("dve", lambda e, orf2=orf2, mx=mx, gg=gg: e.tensor_tensor(out=mx.t[:, 1024:2048], in0=orf2, in1=gg.t[:], op=ALU.mult), r=[orf, gg], w=[mx])

              def tail(mx=mx, o=o):
                  tTa, tTb = nxt(stgT, "stgT"), nxt(stgT, "stgT")
                  transpose_cols(mx, lambda g0, n, tTa=tTa: tTa.t[:, g0:g0 + n, :], 8, tTa)
                  transpose_cols(mx, lambda g0, n, tTb=tTb: tTb.t[:, g0:g0 + n, :], 8, tTb, src_off=1024)
                  bmA, bmB = Buf("mA"), Buf("mB")
                  kb.dma("pool", mixT_s[o, :, 0:8, :], tTa.t[:], r=[tTa], w=[bmA])
                  kb.dma("pool", mixT_s[o, :, 8:16, :], tTb.t[:], r=[tTb], w=[bmB])
                  B_mixT[o] = [bmA, bmB]
              tail_q.append(tail)
      while tail_q:
          tail_q.pop(0)()

    try:
        chk(30)
        p3_body()
    except _Stop:
        kb.finish()
        kb.close_all()
        return kb
    if getattr(cfg, "stop", 99) == 3:
        kb.finish()
        kb.close_all()
        return kb
    kb.pop()
    kb.pop()
    kb.push()
    NB4 = 4
    load_ln(1)
    stgT = [kb.sb("stgT%d" % i, [128, 8, 128], BF16) for i in range(3)]
    hf = [kb.sb("hf%d" % i, [128, D], F32) for i in range(2)]
    hb = [kb.sb("hb%d" % i, [128, D], BF16) for i in range(2)]
    wout = kb.sb("wout", [128, KC, D], BF16)
    kb.dma("sp", wout.t[:], wb_out[:, :].rearrange("(k p) c -> p k c", p=128), r=WB_out, w=[wout])
    aT1 = [kb.sb("aT1_%d" % i, [128, KC, 128], BF16) for i in range(2)]
    resid = [kb.sb("resid%d" % i, [128, D], F32) for i in range(2)]
    ybuf = [kb.sb("ybuf%d" % i, [128, D], F32) for i in range(2)]

    def p4_mm(o):
        s2 = o % 2
        mA, mB = B_mixT[o]
        kb.dma("sp", aT1[s2].t[:], mixT_s[o, :, :, :], r=[mA, mB], w=[aT1[s2]])
        kb.dma("sp", resid[s2].t[:], h_s[o, :, :], r=[B_h[o]], w=[resid[s2]])
        for cb in range(4):
            p = PB[cnt["nb"] % 4]
            cnt["nb"] += 1
            for k in range(KC):
                kb.op("pe", lambda e, k=k, p=p: e.matmul(p.t[:], aT1[s2].t[:, k, :], wout.t[:, k, cb * 512:(cb + 1) * 512], start=(k == 0), stop=(k == KC - 1)),
                      r=[aT1[s2], wout], w=[p], inc=(k == KC - 1))
            kb.op("dve", lambda e, p=p: e.scalar_tensor_tensor(out=ybuf[s2].t[:, cb * 512:(cb + 1) * 512], in0=resid[s2].t[:, cb * 512:(cb + 1) * 512],
                                                              scalar=ALPHA, in1=p.t[:], op0=ALU.mult, op1=ALU.add), r=[resid[s2], p], w=[ybuf[s2]])
        layer_norm(ybuf[s2], hf[s2], hb[s2])
        kb.dma("pool", h1_s[o, :, :], hf[s2].t[:], r=[hf[s2]], w=[B_h1[o]])

    def p4_tr(o):
        hb_ = hb[o % 2]
        tTa, tTb = nxt(stgT, "stgT"), nxt(stgT, "stgT")
        transpose_cols(hb_, lambda g0, n, tTa=tTa: tTa.t[:, g0:g0 + n, :], 8, tTa)
        transpose_cols(hb_, lambda g0, n, tTb=tTb: tTb.t[:, g0:g0 + n, :], 8, tTb, src_off=1024)
        bmA, bmB = Buf("hA"), Buf("hB")
        kb.dma("pool", h1T_s[o, :, 0:8, :], tTa.t[:], r=[tTa], w=[bmA])
        kb.dma("pool", h1T_s[o, :, 8:16, :], tTb.t[:], r=[tTb], w=[bmB])
        B_h1T[o] = [bmA, bmB]

    for i in range(NOWN + 1):
        if i < NOWN:
            p4_mm(i)
        if i >= 1:
            p4_tr(i - 1)

    kb.pop()
    kb.push()
    load_ln(2)
    aTin = kb.sb("aTin", [128, KC, NB4 * 128], BF16)
    resid1 = kb.sb("resid1", [128, D], F32)
    ybuf = [kb.sb("ybuf%d" % i, [128, D], F32) for i in range(NB4)]
    wsl = [kb.sb("wgu%d" % i, [128, KC, 256], BF16) for i in range(3)]
    aT = kb.sb("aT", [128, FC, NB4 * 128], BF16)
    sg = [kb.sb("sg%d" % i, [128, 512], F32) for i in range(2)]
    wdn = [kb.sb("wdn%d" % i, [128, FC, 256], BF16) for i in range(2)]
    cnt.update(sg=0, wd=0)
    NG = 2
    aTb = bufs("aTinb", NB4)
    blocks5 = [list(range(b0, min(NOWN, b0 + NB4))) for b0 in range(0, NOWN, NB4)]

    def load_aTin(own, q):
        for ti, o in enumerate(own):
            kb.dma(q, aTin.t[:, :, ti * 128:(ti + 1) * 128], h1T_s[o, :, :, :], r=B_h1T[o], w=[aTb[ti]])

    def ln2_tile(o, ti):
        kb.dma("sp", resid1.t[:], h1_s[o, :, :], r=[B_h1[o]], w=[resid1])
        kb.op("dve", lambda e, ti=ti: e.scalar_tensor_tensor(out=ybuf[ti].t[:], in0=resid1.t[:], scalar=ALPHA, in1=ybuf[ti].t[:], op0=ALU.mult, op1=ALU.add),
              r=[resid1, ybuf[ti]], w=[ybuf[ti]])
        layer_norm(ybuf[ti], ybuf[ti], None)
        kb.dma("pool", y_out[o * 128:(o + 1) * 128, :], ybuf[ti].t[:], r=[ybuf[ti]])

    pend = []
    load_aTin(blocks5[0], "sp")
    for bi5, own in enumerate(blocks5):
        nt = len(own) * 128
        ab = aTb[:len(own)]
        for gi, f0 in enumerate(range(0, FC, NG)):
            ng = min(NG, FC - f0)
            wg_, wu_ = nxt(wsl, "w"), nxt(wsl, "w")
            kb.dma("sp", wg_.t[:, :, 0:ng * 128], wb_g[:, f0 * 128:(f0 + ng) * 128].rearrange("(k p) c -> p k c", p=128), r=WB_g, w=[wg_])
            kb.dma("sp", wu_.t[:, :, 0:ng * 128], wb_u[:, f0 * 128:(f0 + ng) * 128].rearrange("(k p) c -> p k c", p=128), r=WB_u, w=[wu_])
            for fl in range(ng):
                fc = f0 + fl
                pg, pu = PB[(cnt["nb"] % 2) * 2], PB[(cnt["nb"] % 2) * 2 + 1]
                cnt["nb"] += 1
                for (pw, wt) in ((pg, wg_), (pu, wu_)):
                    for k in range(KC):
                        kb.op("pe", lambda e, k=k, pw=pw, wt=wt, fl=fl: e.matmul(pw.t[:, 0:nt], wt.t[:, k, fl * 128:(fl + 1) * 128], aTin.t[:, k, 0:nt],
                                                                               start=(k == 0), stop=(k == KC - 1)),
                              r=[wt] + ab, w=[pw], inc=(k == KC - 1))
                sg_ = nxt(sg, "sg")
                kb.op("act", lambda e, sg_=sg_, pg=pg: e.activation(out=sg_.t[:, 0:nt], in_=pg.t[:, 0:nt], func=AF.Silu), r=[pg], w=[sg_])
                kb.op("dve", lambda e, sg_=sg_, pu=pu, fc=fc: e.tensor_tensor(out=aT.t[:, fc, 0:nt], in0=sg_.t[:, 0:nt], in1=pu.t[:, 0:nt], op=ALU.mult),
                      r=[sg_, pu], w=[aT])
            if pend and gi >= 1 and gi % 3 == 1:
                ln2_tile(*pend.pop(0))
        while pend:
            ln2_tile(*pend.pop(0))
        if bi5 + 1 < len(blocks5):
            load_aTin(blocks5[bi5 + 1], "pool")
        for cb in range(8):
            wd_ = nxt(wdn, "wd")
            kb.dma("sp", wd_.t[:], wb_d[:, cb * 256:(cb + 1) * 256].rearrange("(k p) c -> p k c", p=128), r=WB_d, w=[wd_])
            for ti, o in enumerate(own):
                p = PB[4 + cnt["nb"] % 2]
                cnt["nb"] += 1
                for fc in range(FC):
                    kb.op("pe", lambda e, fc=fc, ti=ti, p=p: e.matmul(p.t[:, 0:256], aT.t[:, fc, ti * 128:(ti + 1) * 128], wd_.t[:, fc, :], start=(fc == 0), stop=(fc == FC - 1)),
                          r=[aT, wd_], w=[p], inc=(fc == FC - 1))
                kb.op("act", lambda e, ti=ti, cb=cb, p=p: e.activation(out=ybuf[ti].t[:, cb * 256:(cb + 1) * 256], in_=p.t[:, 0:256], func=AF.Copy), r=[p], w=[ybuf[ti]])
        pend = [(o, ti) for ti, o in enumerate(own)]
    while pend:
        ln2_tile(*pend.pop(0))
    kb.finish()
    kb.close_all()
    return kb


def na_mask(R, j, dlt):
    kt = j + dlt
    if kt < 0 or kt >= R // 2:
        return None
    m = np.full((128, 128), NEG, np.float32)
    kr = np.arange(128) // 64 + 2 * kt
    kc = np.arange(128) % 64
    for q in range(128):
        qrow, qc = 2 * j + q // 64, q % 64
        rs = min(max(qrow - 4, 0), R - 8)
        cs = min(max(qc - 8, 0), 64 - 16)
        ok = (kr >= rs) & (kr < rs + 8) & (kc >= cs) & (kc < cs + 16)
        m[ok, q] = 0.0
    return m


def prompt_masks(NT_P):
    R = 2 * NT_P
    var, tabs, keyd = {}, [], {}
    for j in range(NT_P):
        for dlt in range(-3, 4):
            m = na_mask(R, j, dlt)
            if m is None or not (m == 0).any():
                continue
            kk = m.tobytes()
            if kk not in keyd:
                keyd[kk] = len(tabs)
                tabs.append(m)
            var[(j, dlt)] = keyd[kk]
    return var, np.stack(tabs, 1)


def host_inputs(cfg, core, inp):
    NT_P, NT_S, NS_OWN, NWIN, NSL = cfg.NT_P, cfg.NT_S, cfg.NS_OWN, cfg.NWIN, cfg.NSL
    f32 = np.float32
    xsamp = inp["x_sample"][0]
    m = {}
    xm = np.zeros((128, D), f32)
    xm[:16] = inp["meta_tokens"]
    m["xm"] = xm
    m["xp"] = np.ascontiguousarray(inp["x_prompt"][core])
    g0 = NS_OWN * core - 3
    slots = [g if 0 <= g < NT_S else -1 for g in range(g0, g0 + NWIN)]
    rest = [g for g in range(NT_S) if g not in slots]
    slots += rest + [-1] * (NSL - NWIN - len(rest))
    xs = np.zeros((NSL * 128, D), f32)
    for s, g in enumerate(slots):
        if g >= 0:
            xs[s * 128:(s + 1) * 128] = xsamp[g * 128:(g + 1) * 128]
    m["xs"] = xs
    m["w_in"], m["w_out"] = inp["w_in"][0], inp["w_out"][0]
    m["w_g"], m["w_u"], m["w_d"] = inp["w_ffn_gate"][0], inp["w_ffn_up"][0], inp["w_ffn_down"][0]
    rep = lambda v, n=128: np.ascontiguousarray(np.broadcast_to(np.asarray(v, f32).reshape(1, -1), (n, np.asarray(v).size)))
    m["lnt"] = np.stack([rep(inp["ln_in_g"]), rep(inp["ln_in_b"]), rep(inp["ln1_g"][0]), rep(inp["ln1_b"][0]),
                         rep(inp["ln2_g"][0]), rep(inp["ln2_b"][0])], 0)
    m["gng"] = rep(inp["ret_gn_g"][0])
    m["dec"] = rep(np.concatenate([inp["ret_decay_f"][0], inp["ret_decay_b"][0]]))
    m["ident"] = np.eye(128, dtype=f32)
    half = 64
    inv = (f32(10000.0) ** (-np.arange(half, dtype=f32) / f32(half))).astype(f32)
    pos = np.zeros((cfg.NT1, 128), f32)
    pos[0, :16] = np.arange(16)
    for j in range(NT_P):
        pos[1 + j] = 16 + 128 * j + np.arange(128)
    for s, g in enumerate(slots):
        if g >= 0:
            pos[1 + NT_P + s] = 16 + 128 * g + np.arange(128)
    ang = (pos[:, :, None] * inv[None, None, :]).astype(f32)
    cc, ss = np.cos(ang).astype(f32), np.sin(ang).astype(f32)
    ksc = f32(128.0 ** -0.5)
    m["rope"] = np.ascontiguousarray(np.stack([cc, ss, cc * ksc, ss * ksc], 2)).astype(f32)
    jj, ii = np.arange(128)[:, None], np.arange(128)[None, :]
    m["atc"] = np.ascontiguousarray(np.stack([np.maximum(ii - jj, 0), (ii >= jj), np.maximum(jj - ii, 0), (jj > ii)], 1)).astype(f32)
    i_ = np.arange(128, dtype=f32)
    m["xz"] = np.stack([i_ + 1, 128 - i_, 127 - i_, i_], 1).astype(f32)
    cs_, ce_ = NS_OWN * core + 1, NS_OWN * core + NS_OWN
    sic = np.zeros((128, 4, 1 + NSL), f32)
    sic[:16, 0, 0] = (C - 1 - (112 + np.arange(16))) + C * (cs_ - 1)
    sic[:16, 1, 0] = 1.0
    for s, g in enumerate(slots):
        if g < 0:
            continue
        q = g + 1
        if q < cs_:
            sic[:, 0, 1 + s] = (C - 1 - i_) + C * (cs_ - 1 - q)
            sic[:, 1, 1 + s] = 1.0
        if q > ce_:
            sic[:, 2, 1 + s] = i_ + C * (q - ce_ - 1)
            sic[:, 3, 1 + s] = 1.0
    m["sic"] = sic
    sip = np.zeros((128, 2), f32)
    sip[:16, 0] = C - 1 - (112 + np.arange(16))
    sip[:16, 1] = 1.0
    m["sip"] = sip
    rpb = inp["na_rpb"][0]
    kr_, kc_ = np.arange(128) // 64, np.arange(128) % 64
    bu = np.zeros((128, 7, 16, 128), f32)
    for dlt in range(-3, 4):
        dr = 2 * dlt + kr_[:, None] - kr_[None, :]
        dc = kc_[:, None] - kc_[None, :]
        ok = (np.abs(dr) <= 7) & (np.abs(dc) <= 15)
        g = rpb[:, np.clip(dr + 7, 0, 14), np.clip(dc + 15, 0, 30)]
        bu[:, dlt + 3] = np.where(ok[None], g, 0.0).transpose(1, 0, 2)
    m["biasu"] = bu
    m["maskp"] = cfg.maskp
    ms = np.full((128, NS_OWN * 7, 128), NEG, f32)
    for jl in range(NS_OWN):
        for dlt in range(-3, 4):
            mm = na_mask(2 * NT_S, NS_OWN * core + jl, dlt)
            if mm is not None:
                ms[:, jl * 7 + dlt + 3] = mm
    m["masks"] = ms
    return {k: np.ascontiguousarray(v, dtype=f32) for k, v in m.items()}


_CACHE = {}


def run(cfg, inp):
    cfg.pvar, cfg.maskp = prompt_masks(cfg.NT_P)
    cfg.NVP = cfg.maskp.shape[1]
    kb = build(cfg)
    in_maps = [host_inputs(cfg, c, inp) for c in range(8)]
    res = run_bass_kernel_spmd(kb.nc, in_maps, core_ids=list(range(8)))
    return res.results


def kernel(**inputs):
    inp = {k: np.asarray(v) for k, v in inputs.items()}
    cfg = Cfg()
    res = run(cfg, inp)
    NT_P, NS_OWN = cfg.NT_P, cfg.NS_OWN
    yp = np.stack([res[c]["y"][:NT_P * 128] for c in range(8)], 0).astype(np.float32)
    ys = np.concatenate([res[c]["y"][NT_P * 128:] for c in range(8)], 0)[None].astype(np.float32)
    return yp, ys
```
